# Optimizing a Trainium2 kernel written in Bass

```python
import math
import jax, jax.numpy as jnp
from jax import lax
import numpy as np

D_MODEL = 1024
BATCH = 8
SEQ = 4096
DEPTH = 2

GRID_W = 64
CTX_LEN = 256
Q_BLOCK = 128
EPS = 1e-6
ROPE_THETA = 10000.0
DA_HEADS = 4
DA_QK_DIM = 64
DA_V_DIM = 2 * DA_QK_DIM
DA_WIDTH = DA_HEADS * DA_V_DIM
GQ_HEADS = 8
GQ_KV_HEADS = 2
GQ_GROUP = GQ_HEADS // GQ_KV_HEADS
GQ_DIM = 128
GQ_WIDTH = GQ_HEADS * GQ_DIM
HY_WIDTH = 512
HY_ORDER = 2
HY_SHORT = 3
HY_EMB = 33
HY_BANDS = (HY_EMB - 1) // 2
HY_FFN = 64
HY_TARGET = 1e-2
HY_FAST_PCT = 0.3
HY_SLOW_PCT = 1.5
HY_MAX_DECAY = math.log(HY_TARGET) / HY_FAST_PCT
HY_MIN_DECAY = math.log(HY_TARGET) / HY_SLOW_PCT
HY_U = (HY_ORDER + 1) * HY_WIDTH
D_MIX = DA_WIDTH + GQ_WIDTH + HY_WIDTH
COL_SIZES = (DA_HEADS * 2 * DA_QK_DIM, DA_HEADS * 2 * DA_QK_DIM, DA_WIDTH, DA_WIDTH,
             GQ_WIDTH, GQ_KV_HEADS * GQ_DIM, GQ_KV_HEADS * GQ_DIM, GQ_WIDTH,
             HY_U, HY_WIDTH)
D_IN = sum(COL_SIZES)

kernel_name = 'hymba_style_diffattn_gqa_hyena_dit'


def _rms(x, g):
    xf = x.astype(jnp.float32)
    y = xf * lax.rsqrt(jnp.mean(xf * xf, axis=-1, keepdims=True) + EPS)
    return (y * g.astype(jnp.float32)).astype(x.dtype)


def _axial_tables(L, head_dim):
    rows_n = L // GRID_W
    row = jnp.repeat(jnp.arange(rows_n), GRID_W).astype(jnp.float32)
    col = jnp.tile(jnp.arange(GRID_W), rows_n).astype(jnp.float32)
    axis_dim = head_dim // 2
    inv = ROPE_THETA ** (-jnp.arange(0, axis_dim, 2, dtype=jnp.float32) / axis_dim)
    ang_r = row[:, None] * inv[None]
    ang_c = col[:, None] * inv[None]
    return (jnp.cos(ang_r), jnp.sin(ang_r), jnp.cos(ang_c), jnp.sin(ang_c))


def _rope_1d(x, cos, sin):
    shape = (x.shape[1],) + (1,) * (x.ndim - 3) + (cos.shape[-1],)
    cos = cos.reshape(shape)
    sin = sin.reshape(shape)
    x1, x2 = jnp.split(x.astype(jnp.float32), 2, axis=-1)
    return jnp.concatenate([x1 * cos - x2 * sin, x2 * cos + x1 * sin], axis=-1)


def _rope_2d(x, tabs):
    cr, sr, cc, sc = tabs
    xr, xc = jnp.split(x, 2, axis=-1)
    return jnp.concatenate([_rope_1d(xr, cr, sr), _rope_1d(xc, cc, sc)], axis=-1).astype(x.dtype)


def _sweep_queries(fn, q):
    B, L = q.shape[0], q.shape[1]
    nb = L // Q_BLOCK
    qb = jnp.moveaxis(q.reshape((B, nb, Q_BLOCK) + q.shape[2:]), 1, 0)
    out = jnp.moveaxis(lax.map(fn, qb), 0, 1)
    return out.reshape((B, L) + out.shape[3:])


def _diff_attend(q, k, v, lam):
    scale = DA_QK_DIM ** -0.5
    def body(qb):
        s = jnp.einsum('bqhcd,bkhcd->bhcqk', qb, k).astype(jnp.float32) * scale
        p = jax.nn.softmax(s, axis=-1)
        a = (p[:, :, 0] - lam * p[:, :, 1]).astype(v.dtype)
        return jnp.einsum('bhqk,bkhe->bqhe', a, v)
    return _sweep_queries(body, q)


def _gqa_attend(q, k, v):
    scale = GQ_DIM ** -0.5
    def body(qb):
        s = jnp.einsum('bqgrd,bkgd->bgrqk', qb, k).astype(jnp.float32) * scale
        p = jax.nn.softmax(s, axis=-1).astype(v.dtype)
        return jnp.einsum('bgrqk,bkgd->bqgrd', p, v)
    return _sweep_queries(body, q)


def _hyena_filter(L, w1, b1, w2, b2, w3, b3, w4, freq):
    f32 = jnp.float32
    t = jnp.linspace(0.0, 1.0, L, dtype=f32)[:, None]
    w = 2.0 * math.pi * jnp.arange(L, dtype=f32)[:, None] / L
    f = jnp.linspace(1e-4, HY_BANDS - 1, HY_BANDS, dtype=f32)[None]
    z = jnp.concatenate([t, jnp.cos(f * w), -jnp.sin(f * w)], axis=-1)
    fr = freq.astype(f32)
    h = jnp.sin(fr * (z @ w1.astype(f32) + b1.astype(f32)))
    h = jnp.sin(fr * (h @ w2.astype(f32) + b2.astype(f32)))
    h = jnp.sin(fr * (h @ w3.astype(f32) + b3.astype(f32)))
    h = h @ w4.astype(f32)
    deltas = jnp.linspace(HY_MIN_DECAY, HY_MAX_DECAY, HY_WIDTH, dtype=f32)
    decay = jnp.exp(-t * jnp.abs(deltas)[None])
    h = h.reshape(L, 2, HY_WIDTH) * decay[:, None, :]
    fwd, bwd = h[:, 0], h[:, 1]
    full = jnp.concatenate([fwd, jnp.zeros((1, HY_WIDTH), f32), bwd[:0:-1]], axis=0)
    return full * lax.rsqrt(jnp.sum(full * full, axis=0, keepdims=True) + EPS)


def _long_conv(u, filt, skip):
    L = u.shape[1]
    n = 2 * L
    uf = u.astype(jnp.float32)
    U = jnp.fft.rfft(uf, n=n, axis=1)
    Hf = jnp.fft.rfft(filt, n=n, axis=0)
    y = jnp.fft.irfft(U * Hf[None], n=n, axis=1)[:, :L]
    return (y + uf * skip.astype(jnp.float32)).astype(u.dtype)


def _short_conv(u, w, b):
    C = u.shape[-1]
    y = lax.conv_general_dilated(u, w[:, None, :].astype(u.dtype), window_strides=(1,),
                                 padding=[(1, 1)], dimension_numbers=('NWC', 'WIO', 'NWC'),
                                 feature_group_count=C)
    return y + b


def _hyena(u, filt, short_w, short_b, skip):
    u = _short_conv(u, short_w, short_b)
    x0, x1, v = jnp.split(u, 3, axis=-1)
    return x0 * _long_conv(v * x1, filt, skip)


def _prep(p, q_g, k_g):
    qa, ka, va, ga, qg, kg, vg, gg, uh, gh = jnp.split(p, np.cumsum(COL_SIZES)[:-1].tolist(), axis=-1)
    B, L = p.shape[0], p.shape[1]
    qa = qa.reshape(B, L, DA_HEADS, 2, DA_QK_DIM)
    ka = ka.reshape(B, L, DA_HEADS, 2, DA_QK_DIM)
    va = va.reshape(B, L, DA_HEADS, DA_V_DIM)
    qg = _rms(qg.reshape(B, L, GQ_KV_HEADS, GQ_GROUP, GQ_DIM), q_g)
    kg = _rms(kg.reshape(B, L, GQ_KV_HEADS, GQ_DIM), k_g)
    vg = vg.reshape(B, L, GQ_KV_HEADS, GQ_DIM)
    return qa, ka, va, ga, qg, kg, vg, gg, uh, gh


def _mixer_output(qa, ka, va, ga, qg, kg, vg, gg, uh, gh, filt, lam, lam_init,
                  da_subln_g, gq_out_g, hy_short_w, hy_short_b, hy_bias, hy_out_g, w_out):
    B, L = qa.shape[0], qa.shape[1]
    oa = _rms(_diff_attend(qa, ka, va, lam), da_subln_g) * (1.0 - lam_init)
    oa = oa.reshape(B, L, DA_WIDTH)
    ob = _rms(_gqa_attend(qg, kg, vg).reshape(B, L, GQ_WIDTH), gq_out_g)
    oc = _rms(_hyena(uh, filt, hy_short_w, hy_short_b, hy_bias), hy_out_g)
    y = jnp.concatenate([oa * jax.nn.silu(ga), ob * jax.nn.silu(gg), oc * jax.nn.silu(gh)], axis=-1)
    return y @ w_out


def setup_inputs(seed: int = 0) -> dict:
    key = jax.random.key(seed)
    ks = jax.random.split(key, 27)
    f32 = jnp.float32
    def nrm(k, shape, s):
        return s * jax.random.normal(k, shape, f32)
    def gain(k, shape):
        return 1.0 + 0.02 * jax.random.normal(k, shape, f32)
    D = D_MODEL
    return {
        'x': nrm(ks[0], (BATCH, SEQ, D), 1.0),
        'c': nrm(ks[1], (BATCH, D), 1.0),
        'ctx': nrm(ks[2], (BATCH, CTX_LEN, D), 1.0),
        'c_ctx': nrm(ks[3], (D,), 1.0),
        'ada_w': nrm(ks[4], (DEPTH, D, 3 * D), D ** -0.5),
        'ada_b': nrm(ks[5], (DEPTH, 3 * D), 0.01),
        'norm_g': gain(ks[6], (DEPTH, D)),
        'w_in': nrm(ks[7], (DEPTH, D, D_IN), D ** -0.5),
        'w_out': nrm(ks[8], (DEPTH, D_MIX, D), D_MIX ** -0.5),
        'da_lambda': nrm(ks[9], (DEPTH, 4, DA_QK_DIM), 0.1),
        'da_subln_g': gain(ks[10], (DEPTH, DA_V_DIM)),
        'gq_q_g': gain(ks[11], (DEPTH, GQ_DIM)),
        'gq_k_g': gain(ks[12], (DEPTH, GQ_DIM)),
        'gq_out_g': gain(ks[13], (DEPTH, GQ_WIDTH)),
        'hy_short_w': nrm(ks[14], (DEPTH, HY_SHORT, HY_U), HY_SHORT ** -0.5),
        'hy_short_b': nrm(ks[15], (DEPTH, HY_U), 0.01),
        'hy_w1': nrm(ks[16], (DEPTH, HY_EMB, HY_FFN), HY_EMB ** -0.5),
        'hy_b1': nrm(ks[17], (DEPTH, HY_FFN), 0.1),
        'hy_w2': nrm(ks[18], (DEPTH, HY_FFN, HY_FFN), HY_FFN ** -0.5),
        'hy_b2': nrm(ks[19], (DEPTH, HY_FFN), 0.1),
        'hy_w3': nrm(ks[20], (DEPTH, HY_FFN, HY_FFN), HY_FFN ** -0.5),
        'hy_b3': nrm(ks[21], (DEPTH, HY_FFN), 0.1),
        'hy_w4': nrm(ks[22], (DEPTH, HY_FFN, 2 * HY_WIDTH), HY_FFN ** -0.5),
        'hy_freq': gain(ks[23], (DEPTH, HY_FFN)),
        'hy_bias': nrm(ks[24], (DEPTH, HY_WIDTH), 1.0),
        'hy_out_g': gain(ks[25], (DEPTH, HY_WIDTH)),
        'final_g': gain(ks[26], (D,)),
    }


def reference(x, c, ctx, c_ctx, ada_w, ada_b, norm_g, w_in, w_out, da_lambda, da_subln_g,
              gq_q_g, gq_k_g, gq_out_g, hy_short_w, hy_short_b, hy_w1, hy_b1, hy_w2, hy_b2,
              hy_w3, hy_b3, hy_w4, hy_freq, hy_bias, hy_out_g, final_g):
    L = x.shape[1]
    Lc = ctx.shape[1]
    tab_da = _axial_tables(L, DA_QK_DIM)
    tab_gq = _axial_tables(L, GQ_DIM)
    xc = ctx
    for l in range(DEPTH):
        update_ctx = l < DEPTH - 1
        lam_init = 0.8 - 0.6 * math.exp(-0.3 * l)
        lmb = da_lambda[l].astype(jnp.float32)
        lam = jnp.exp(jnp.sum(lmb[0] * lmb[1])) - jnp.exp(jnp.sum(lmb[2] * lmb[3])) + lam_init
        shift, scale, gate = jnp.split(jax.nn.silu(c) @ ada_w[l] + ada_b[l], 3, axis=-1)
        shift_c, scale_c, gate_c = jnp.split(jax.nn.silu(c_ctx) @ ada_w[l] + ada_b[l], 3, axis=-1)
        h = _rms(x, norm_g[l]) * (1.0 + scale[:, None]) + shift[:, None]
        hc = _rms(xc, norm_g[l]) * (1.0 + scale_c) + shift_c
        qa, ka, va, ga, qg, kg, vg, gg, uh, gh = _prep(h @ w_in[l], gq_q_g[l], gq_k_g[l])
        qa_c, ka_c, va_c, ga_c, qg_c, kg_c, vg_c, gg_c, uh_c, gh_c = _prep(hc @ w_in[l], gq_q_g[l], gq_k_g[l])
        ka_all = jnp.concatenate([ka_c, _rope_2d(ka, tab_da)], axis=1)
        va_all = jnp.concatenate([va_c, va], axis=1)
        kg_all = jnp.concatenate([kg_c, _rope_2d(kg, tab_gq)], axis=1)
        vg_all = jnp.concatenate([vg_c, vg], axis=1)
        filt = _hyena_filter(L, hy_w1[l], hy_b1[l], hy_w2[l], hy_b2[l], hy_w3[l], hy_b3[l], hy_w4[l], hy_freq[l])
        y = _mixer_output(_rope_2d(qa, tab_da), ka_all, va_all, ga, _rope_2d(qg, tab_gq), kg_all, vg_all, gg,
                          uh, gh, filt, lam, lam_init, da_subln_g[l], gq_out_g[l],
                          hy_short_w[l], hy_short_b[l], hy_bias[l], hy_out_g[l], w_out[l])
        if update_ctx:
            filt_c = _hyena_filter(Lc, hy_w1[l], hy_b1[l], hy_w2[l], hy_b2[l], hy_w3[l], hy_b3[l], hy_w4[l], hy_freq[l])
            yc = _mixer_output(qa_c, ka_c, va_c, ga_c, qg_c, kg_c, vg_c, gg_c, uh_c, gh_c, filt_c, lam, lam_init,
                               da_subln_g[l], gq_out_g[l], hy_short_w[l], hy_short_b[l], hy_bias[l],
                               hy_out_g[l], w_out[l])
            xc = xc + gate_c * yc
        x = x + gate[:, None] * y
    return _rms(x, final_g)
```

```python
import math
import contextlib
import numpy as np
import ml_dtypes
import concourse.bass as bass
import concourse.mybir as mybir
from concourse.bass_utils import run_bass_kernel_spmd

F32 = mybir.dt.float32
BF16 = mybir.dt.bfloat16
AF = mybir.ActivationFunctionType
ALU = mybir.AluOpType

D = 1024
L = 4096
LC = 256
T = L + LC
NT = T // 128
DEPTH = 2
DIN = 6656
DMIX = 2048
EPS = 1e-6
BLK = [(0, 256)] + [(256 + 512 * i, 768 + 512 * i) for i in range(8)]
DEBUG = []
NLAYERS = DEPTH
STOP = None
NCORES = 8
NJ = 8
NB = 9
NTA = NT
SUB = 9
VAR = 0


class _Stop(Exception):
    pass


class Buf:
    __slots__ = ("w", "r", "x")

    def __init__(self, x=False):
        self.w = {}
        self.r = {}
        self.x = x


class Prog:
    def __init__(self, nc, n_dma_sems=32):
        self.nc = nc
        self.eng = {"pe": nc.tensor, "act": nc.scalar, "dve": nc.vector, "pool": nc.gpsimd, "sp": nc.sync}
        self.sem = {e: nc.alloc_semaphore("s_" + e) for e in self.eng}
        self.cnt = {e: 0 for e in self.eng}
        self.seen = {e: {} for e in self.eng}
        self.dsem = [nc.alloc_semaphore("d%d" % i) for i in range(n_dma_sems)]
        self.dcnt = [0] * n_dma_sems
        self.dnext = 0
        self.semobj = {}
        for e in self.eng:
            self.semobj[("e", e)] = self.sem[e]
        for i, s in enumerate(self.dsem):
            self.semobj[("d", i)] = s

    def _wait(self, e, key, val):
        if val <= 0:
            return
        seen = self.seen[e]
        if seen.get(key, 0) >= val:
            return
        seen[key] = val
        self.eng[e].wait_ge(self.semobj[key], val)

    def _deps(self, e, reads, writes, disjoint):
        deps = {}
        for b in reads:
            for k, v in b.w.items():
                if deps.get(k, 0) < v:
                    deps[k] = v
            if b.x:
                for k, v in b.r.items():
                    if k != ("e", e) and deps.get(k, 0) < v:
                        deps[k] = v
        for b in writes:
            if not disjoint:
                for k, v in b.w.items():
                    if deps.get(k, 0) < v:
                        deps[k] = v
            for k, v in b.r.items():
                if k == ("e", e):
                    continue
                if deps.get(k, 0) < v:
                    deps[k] = v
        if e == "pe":
            deps.pop(("e", "pe"), None)
        for k, v in deps.items():
            self._wait(e, k, v)

    def _mark(self, key, val, reads, writes, disjoint):
        for b in reads:
            if b.r.get(key, 0) < val:
                b.r[key] = val
        for b in writes:
            if not disjoint:
                b.w = {key: val}
            else:
                if b.w.get(key, 0) < val:
                    b.w[key] = val
            b.r = {}

    def op(self, e, fn, reads=(), writes=(), disjoint=False):
        self._deps(e, reads, writes, disjoint)
        ins = fn(self.eng[e])
        self.cnt[e] += 1
        ins.then_inc(self.sem[e], 1)
        self._mark(("e", e), self.cnt[e], reads, writes, disjoint)
        return ins

    def dma(self, out, in_, reads=(), writes=(), q="sp", disjoint=False, **kw):
        self._deps(q, reads, writes, disjoint)
        i = self.dnext
        self.dnext = (self.dnext + 1) % len(self.dsem)
        key = ("d", i)
        self._wait(q, key, self.dcnt[i])
        ins = self.eng[q].dma_start(out=out, in_=in_, **kw)
        self.dcnt[i] += 16
        ins.then_inc(self.dsem[i], 16)
        self._mark(key, self.dcnt[i], reads, writes, disjoint)
        return ins

    def barrier(self):
        for e in self.eng:
            for f in self.eng:
                if f != e:
                    self._wait(e, ("e", f), self.cnt[f])
            for i in range(len(self.dsem)):
                self._wait(e, ("d", i), self.dcnt[i])

    def act(self, out, in_, func, reads, writes, **kw):
        return self.op("act", lambda e: e.activation(out=out, in_=in_, func=func, **kw), reads, writes)

    def mm(self, out, lhsT, rhs, start, stop, reads, writes):
        return self.op("pe", lambda e: e.matmul(out, lhsT=lhsT, rhs=rhs, start=start, stop=stop), reads, writes)

    def tt(self, out, in0, in1, op, reads, writes, eng="dve"):
        return self.op(eng, lambda e: e.tensor_tensor(out=out, in0=in0, in1=in1, op=op), reads, writes)

    def ts(self, out, in0, s1, s2, op0, op1, reads, writes, eng="dve"):
        if op1 is None:
            return self.op(eng, lambda e: e.tensor_scalar(out=out, in0=in0, scalar1=s1, scalar2=None, op0=op0), reads, writes)
        return self.op(eng, lambda e: e.tensor_scalar(out=out, in0=in0, scalar1=s1, scalar2=s2, op0=op0, op1=op1), reads, writes)

    def stt(self, out, in0, scalar, in1, op0, op1, reads, writes):
        return self.op("dve", lambda e: e.scalar_tensor_tensor(out=out, in0=in0, scalar=scalar, in1=in1, op0=op0, op1=op1), reads, writes)

    def cp(self, out, in_, reads, writes, eng="dve"):
        return self.op(eng, lambda e: e.tensor_copy(out=out, in_=in_), reads, writes)


_UID = [0]


def _u(name):
    _UID[0] += 1
    return "%s_%d" % (name, _UID[0])


class Ring:
    def __init__(self, nc, stack, name, shape, dtype, n):
        self.t = [stack.enter_context(nc.sbuf_tensor(_u(name), shape, dtype)).ap() for i in range(n)]
        self.b = [Buf() for _ in range(n)]
        self.i = 0

    def next(self):
        j = self.i
        self.i = (self.i + 1) % len(self.t)
        return self.t[j], self.b[j]


def build(nlayers=DEPTH):
    nc = bass.Bass("TRN2", target_bir_lowering=False)
    P = Prog(nc)

    def din(name, shape, dt=F32):
        return nc.dram_tensor(name, list(shape), dt, kind="ExternalInput").ap()

    def scr(name, shape, dt):
        kind = "ExternalOutput" if name in DEBUG else "Internal"
        return nc.dram_tensor(name, list(shape), dt, kind=kind).ap(), Buf()

    xin = din("xin", [T, D])
    c_t = din("c_t", [128, 8])
    cc_t = din("cc_t", [128, 8])
    ada_w = din("ada_w", [DEPTH, D, 3 * D])
    ada_b = din("ada_b", [DEPTH, 3 * D])
    norm_g = din("norm_g", [DEPTH, D])
    w_in = din("w_in", [DEPTH, D, DIN])
    w_out = din("w_out", [DEPTH, DMIX, D])
    lamv = din("lamv", [DEPTH, 4, 64])
    cols_in = din("cols", [DEPTH, 128, 32])
    hy_sw = din("hy_sw", [DEPTH, 128, 12, 3])
    hy_w1 = din("hy_w1", [DEPTH, 33, 64])
    hy_w2 = din("hy_w2", [DEPTH, 64, 64])
    hy_w3 = din("hy_w3", [DEPTH, 64, 64])
    hy_w4 = din("hy_w4", [DEPTH, 64, 1024])
    hy_b = din("hy_b", [DEPTH, 64, 4])
    final_g = din("final_g", [D])
    ident_in = din("ident", [128, 128], BF16)
    identf_in = din("identf", [128, 128], F32)
    perm_da_in = din("perm_da", [128, 128], BF16)
    perm_gq_in = din("perm_gq", [128, 128], BF16)
    cos_da = din("cos_da", [128, T])
    sin_da = din("sin_da", [128, T])
    cos_gq = din("cos_gq", [128, T])
    sin_gq = din("sin_gq", [128, T])
    zT = {L: din("zT_l", [33, L]), LC: din("zT_c", [33, LC])}
    decay = {L: din("decay_l", [L, 512]), LC: din("decay_c", [LC, 512])}
    Cf = {L: din("cf_l", [L // 128, 128, L], BF16), LC: din("cf_c", [LC // 128, 128, LC], BF16)}
    Sf = {L: din("sf_l", [L // 128, 128, L], BF16), LC: din("sf_c", [LC // 128, 128, LC], BF16)}
    Ci = {L: din("ci_l", [128, L // 128, L], BF16), LC: din("ci_c", [128, LC // 128, LC], BF16)}
    Si = {L: din("si_l", [128, L // 128, L], BF16), LC: din("si_c", [128, LC // 128, LC], BF16)}
    out = nc.dram_tensor("out", [L, D], F32, kind="ExternalOutput").ap()
    Bout = Buf()

    xs, Bxs = scr("xs", [T, D], F32)
    QTda, BQTda = scr("QTda", [4, 128, T], BF16)
    KTda, BKTda = scr("KTda", [4, 128, T], BF16)
    Vda, BVda = scr("Vda", [T, 512], BF16)
    QTgq, BQTgq = scr("QTgq", [8, 128, T], BF16)
    KTgq, BKTgq = scr("KTgq", [2, 128, T], BF16)
    Vgq, BVgq = scr("Vgq", [T, 256], BF16)
    GT, BGT = scr("GT", [DMIX, T], F32)
    X0T, BX0T = scr("X0T", [512, T], F32)
    UT, BUT = scr("UT", [512, T], F32)
    Utok, BUtok = scr("Utok", [T, 512], BF16)
    Ftok, BFtok = scr("Ftok", [L, 1024], BF16)
    RN, BRN = scr("RN", [128, 512], F32)
    Yf, BYf = scr("Yf", [2, L, 512], BF16)
    YMT, BYMT = scr("YMT", [DMIX, T], BF16)

    ps = [nc.alloc_psum_tensor("ps%d" % i, [128, 512], F32).ap() for i in range(8)]
    Bps = [Buf(True) for _ in range(8)]

    def sb(name, shape, dt=F32):
        return nc.alloc_sbuf_tensor(name, list(shape), dt).ap()

    ident = sb("ident_s", [128, 128], BF16)
    identf = sb("identf_s", [128, 128], F32)
    perm_da = sb("perm_da_s", [128, 128], BF16)
    perm_gq = sb("perm_gq_s", [128, 128], BF16)
    ones_f = sb("ones_f", [128, 128], F32)
    ones_b = sb("ones_b", [128, 128], BF16)
    gate = sb("gate_s", [128, D])
    gate_c = sb("gatec_s", [128, D])
    lam_c = sb("lam_c", [128, 2])
    ct_s = sb("ct_s", [128, 8])
    cct_s = sb("cct_s", [128, 8])
    colsb = sb("colsb", [128, 32])
    Bcols = Buf()
    Bconst = Buf()
    Bgate = Buf()
    Blam = Buf()
    P.dma(ident, ident_in, writes=[Bconst], disjoint=True)
    P.dma(identf, identf_in, writes=[Bconst], disjoint=True)
    P.dma(perm_da, perm_da_in, writes=[Bconst], disjoint=True)
    P.dma(perm_gq, perm_gq_in, writes=[Bconst], disjoint=True)
    P.dma(ct_s, c_t, writes=[Bconst], disjoint=True)
    P.dma(cct_s, cc_t, writes=[Bconst], disjoint=True)
    P.op("dve", lambda e: e.memset(ones_f, 1.0), writes=[Bconst], disjoint=True)
    P.op("dve", lambda e: e.memset(ones_b, 1.0), writes=[Bconst], disjoint=True)
    P.barrier()

    def rstd_from(out_ap, in_ap, scale, reads, writes, tmp, Btmp):
        P.act(tmp, in_ap, AF.Ln, reads, [Btmp], scale=scale, bias=epsc[:in_ap.shape[0], :])
        P.act(out_ap, tmp, AF.Exp, [Btmp], writes, scale=-0.5)

    epsc = sb("epsc", [128, 1])
    P.op("dve", lambda e: e.memset(epsc, EPS), writes=[Bconst], disjoint=True)

    for l in range(nlayers):
      try:
        lam_init = 0.8 - 0.6 * math.exp(-0.3 * l)
        last = l == DEPTH - 1
        xsrc = xin if l == 0 else xs
        P.dma(colsb, cols_in[l], writes=[Bcols])

        with contextlib.ExitStack() as st:
            hT = st.enter_context(nc.sbuf_tensor(_u("hT"), [128, 8, T], BF16)).ap()
            BhT = Buf()
            with contextlib.ExitStack() as st2:
                def s2(name, shape, dt=F32):
                    return st2.enter_context(nc.sbuf_tensor(_u(name), list(shape), dt)).ap()
                lm = s2("lm", [128, 4, 64])
                Blm = Buf()
                P.dma(lm, lamv[l].partition_broadcast(128), writes=[Blm])
                lp = s2("lp", [128, 2, 64])
                ls = s2("ls", [128, 2])
                P.tt(lp[:, 0, :], lm[:, 0, :], lm[:, 1, :], ALU.mult, [Blm], [Blam], )
                P.tt(lp[:, 1, :], lm[:, 2, :], lm[:, 3, :], ALU.mult, [Blm, Blam], [Blam])
                P.op("dve", lambda e: e.tensor_reduce(out=ls, in_=lp, op=ALU.add, axis=mybir.AxisListType.X), [Blam], [Blam])
                P.act(ls, ls, AF.Exp, [Blam], [Blam])
                P.tt(lam_c[:, 0:1], ls[:, 1:2], ls[:, 0:1], ALU.subtract, [Blam], [Blam])
                P.ts(lam_c[:, 0:1], lam_c[:, 0:1], -lam_init, None, ALU.add, None, [Blam], [Blam])

                sc = s2("sc", [128, 8])
                scc = s2("scc", [128, 8])
                rep = s2("rep", [128, 8, 128], BF16)
                repc = s2("repc", [128, 8, 128], BF16)
                Bm = Buf()
                P.act(sc, ct_s, AF.Silu, [Bconst], [Bm])
                P.act(scc, cct_s, AF.Silu, [Bconst, Bm], [Bm])
                for k in range(8):
                    P.ts(rep[:, k, :], ones_f, sc[:, k:k + 1], None, ALU.mult, None, [Bm, Bconst], [Bm])
                    P.ts(repc[:, k, :], ones_f, scc[:, k:k + 1], None, ALU.mult, None, [Bm, Bconst], [Bm])
                modA = s2("modA", [128, 3 * D])
                modC = s2("modC", [128, 3 * D])
                abr = s2("abr", [128, 3 * D])
                ngr = s2("ngr", [128, D])
                Bab = Buf()
                P.dma(abr, ada_b[l].partition_broadcast(128), writes=[Bab], disjoint=True)
                P.dma(ngr, norm_g[l].partition_broadcast(128), writes=[Bab], disjoint=True)
                awr = Ring(nc, st2, "aw", [128, 8, 512], F32, 2)
                awbr = Ring(nc, st2, "awb", [128, 8, 512], BF16, 2)
                Bmod = Buf()
                for n in range(6):
                    aw0, Baw0 = awr.next()
                    P.dma(aw0, ada_w[l][:, n * 512:(n + 1) * 512].rearrange("(k p) c -> p k c", p=128), writes=[Baw0])
                    aw, Baw = awbr.next()
                    P.cp(aw, aw0, [Baw0], [Baw], eng="pool")
                    for wi, (rp, md) in enumerate(((rep, modA), (repc, modC))):
                        pb = wi
                        for k in range(8):
                            P.mm(ps[pb], rp[:, k, :], aw[:, k, :], k == 0, k == 7, [Bm, Baw], [Bps[pb]])
                        P.tt(md[:, n * 512:(n + 1) * 512], ps[pb], abr[:, n * 512:(n + 1) * 512], ALU.add, [Bps[pb], Bab], [Bmod], )
                G1 = s2("G1", [128, D])
                G1c = s2("G1c", [128, D])
                P.stt(G1, modA[:, D:2 * D], 1.0, ngr, ALU.add, ALU.mult, [Bmod, Bab], [Bmod])
                P.stt(G1c, modC[:, D:2 * D], 1.0, ngr, ALU.add, ALU.mult, [Bmod, Bab], [Bmod])
                P.cp(gate, modA[:, 2 * D:3 * D], [Bmod], [Bgate], eng="pool")
                P.cp(gate_c, modC[:, 2 * D:3 * D], [Bmod, Bgate], [Bgate], eng="pool")

                xr = Ring(nc, st2, "xa", [128, D], F32, 2)
                jr = Ring(nc, st2, "junk", [128, D], F32, 1)
                tr = Ring(nc, st2, "tmpa", [128, D], F32, 2)
                hr = Ring(nc, st2, "hb", [128, D], BF16, 2)
                sr = Ring(nc, st2, "ssq", [128, 4], F32, 2)
                for i in range(NTA):
                    xt, Bx = xr.next()
                    P.dma(xt, xsrc[i * 128:(i + 1) * 128, :], reads=[Bxs], writes=[Bx])
                    jk, Bj = jr.next()
                    ss, Bs = sr.next()
                    P.act(jk, xt, AF.Square, [Bx], [Bj, Bs], accum_out=ss[:, 0:1])
                    rstd_from(ss[:, 2:3], ss[:, 0:1], 1.0 / D, [Bs], [Bs], ss[:, 1:2], Bs)
                    tm, Bt = tr.next()
                    g1, sh = (G1c, modC[:, 0:D]) if i < 2 else (G1, modA[:, 0:D])
                    P.stt(tm, xt, ss[:, 2:3], g1, ALU.mult, ALU.mult, [Bx, Bs, Bmod], [Bt])
                    hb, Bh = hr.next()
                    P.tt(hb, tm, sh, ALU.add, [Bt, Bmod], [Bh], eng="pool")
                    pT = ps[2 + (i % 2)].bitcast(BF16)
                    BpT = Bps[2 + (i % 2)]
                    for k in range(8):
                        P.op("pe", lambda e: e.transpose(out=pT[:, k * 128:(k + 1) * 128], in_=hb[:, k * 128:(k + 1) * 128], identity=ident), [Bh, Bconst], [BpT])
                    P.act(hT[:, :, i * 128:(i + 1) * 128], pT.rearrange("p (k t) -> p k t", k=8), AF.Copy, [BpT], [BhT])
            P.barrier()
            if STOP == "A":
                raise _Stop

            with contextlib.ExitStack() as st2:
                def s2(name, shape, dt=F32):
                    return st2.enter_context(nc.sbuf_tensor(_u(name), list(shape), dt)).ap()
                wst = Ring(nc, st2, "wst", [128, 8, 128], F32, 2)
                wbr = Ring(nc, st2, "wb", [128, 8, 128], BF16, 2)
                cosr = Ring(nc, st2, "cosr", [128, 512], F32, 3)
                sinr = Ring(nc, st2, "sinr", [128, 512], F32, 3)
                f1 = Ring(nc, st2, "f1", [128, 512], F32, 3)
                f2 = Ring(nc, st2, "f2", [128, 512], F32, 3)
                f3 = Ring(nc, st2, "f3", [128, 512], F32, 3)
                b1r = Ring(nc, st2, "b1", [128, 512], BF16, 3)
                sqb = Ring(nc, st2, "sqb", [128, 512], BF16, 3)
                obr = Ring(nc, st2, "ob", [128, 512], BF16, 3)
                ofr = Ring(nc, st2, "of", [128, 512], F32, 3)
                gcol = colsb[:, 0:2]
                swc = s2("swc", [128, 12, 3])
                sbc = colsb[:, 19:31]
                Bcol = Buf()
                P.dma(swc, hy_sw[l], reads=[Bcols], writes=[Bcol])
                uraw = s2("uraw", [128, T])
                ucv = s2("ucv", [128, T])
                x1b = s2("x1b", [128, T])
                Buraw, Bucv, Bx1b = Buf(), Buf(), Buf()
                utk = Ring(nc, st2, "utk", [128, 4, 128], BF16, 2)
                mmi = [0]
                pending = []

                def advance():
                    for g in list(pending):
                        try:
                            next(g)
                        except StopIteration:
                            pending.remove(g)

                def run_block(gen):
                    advance()
                    try:
                        next(gen)
                        pending.append(gen)
                    except StopIteration:
                        pass

                def flush():
                    while pending:
                        advance()

                worder = list(range(8)) + list(range(16, 26)) + [12 + i for i in range(4)] + [28 + i for i in range(8)] + [48 + i for i in range(4)]
                for cc_ in range(4):
                    worder += [36 + cc_, 36 + 4 + cc_, 36 + 8 + cc_]
                wpre = {}

                def _load_w(j):
                    ws, Bws = wst.next()
                    P.dma(ws, w_in[l][:, j * 128:(j + 1) * 128].rearrange("(k p) c -> p k c", p=128), writes=[Bws])
                    wb, Bwb = wbr.next()
                    P.cp(wb, ws, [Bws], [Bwb], eng="pool")
                    return wb, Bwb

                def load_w(j):
                    idx = worder.index(j)
                    if j not in wpre:
                        wpre[j] = _load_w(j)
                    r = wpre.pop(j)
                    if idx + 1 < len(worder) and worder[idx + 1] != 36:
                        nj = worder[idx + 1]
                        wpre[nj] = _load_w(nj)
                    return r

                def proj_T(wb, Bwb, t0, t1):
                    pb = mmi[0] % 4
                    mmi[0] += 1
                    n = t1 - t0
                    for k in range(8):
                        P.mm(ps[pb][:, :n], wb[:, k, :], hT[:, k, t0:t1], k == 0, k == 7, [Bwb, BhT], [Bps[pb]])
                    return ps[pb][:, :n], Bps[pb]

                def rope_tail(src, Bsrc, srcb, Bsrcb, perm, ct, Bc, sn, Bsn, dst, Bdst, t0, t1):
                    n = t1 - t0
                    P.mm(ps[4][:, :n], perm, srcb, True, True, [Bconst, Bsrcb], [Bps[4]])
                    a1, Ba1 = f2.next()
                    a2, Ba2 = f3.next()
                    P.tt(a1[:, :n], src, ct[:, :n], ALU.mult, [Bsrc, Bc], [Ba1])
                    P.tt(a2[:, :n], ps[4][:, :n], sn[:, :n], ALU.mult, [Bps[4], Bsn], [Ba2])
                    ob, Bob = obr.next()
                    P.tt(ob[:, :n], a1[:, :n], a2[:, :n], ALU.add, [Ba1, Ba2], [Bob], eng="pool")
                    P.dma(dst[:, t0:t1], ob[:, :n], reads=[Bob], writes=[Bdst], disjoint=True)

                def load_tabs(cosT, sinT, t0, t1):
                    n = t1 - t0
                    ct, Bc = cosr.next()
                    sn, Bsn = sinr.next()
                    P.dma(ct[:, :n], cosT[:, t0:t1], writes=[Bc])
                    P.dma(sn[:, :n], sinT[:, t0:t1], writes=[Bsn])
                    return ct, Bc, sn, Bsn

                def post_da(pp, Bpp, dst, Bdst, t0, t1):
                    n = t1 - t0
                    qb, Bqb = b1r.next()
                    P.act(qb[:, :n], pp, AF.Copy, [Bpp], [Bqb])
                    tabs = load_tabs(cos_da, sin_da, t0, t1)
                    yield
                    rope_tail(pp, Bpp, qb[:, :n], Bqb, perm_da, *tabs, dst, Bdst, t0, t1)

                def post_gq(pp, Bpp, gc, dst, Bdst, t0, t1):
                    n = t1 - t0
                    sq, Bsq = sqb.next()
                    P.act(sq[:, :n], pp, AF.Square, [Bpp], [Bsq])
                    tabs = load_tabs(cos_gq, sin_gq, t0, t1)
                    yield
                    P.mm(ps[5][:, :n], ones_b, sq[:, :n], True, True, [Bconst, Bsq], [Bps[5]])
                    r1, Br1 = f2.next()
                    rs, Brs = f3.next()
                    rstd_from(rs[:, :n], ps[5][:, :n], 1.0 / 128, [Bps[5]], [Brs], r1[:, :n], Br1)
                    qn, Bqn = f1.next()
                    P.stt(qn[:, :n], pp, gc, rs[:, :n], ALU.mult, ALU.mult, [Bpp, Bcols, Brs], [Bqn])
                    qb, Bqb = b1r.next()
                    P.cp(qb[:, :n], qn[:, :n], [Bqn], [Bqb], eng="pool")
                    yield
                    rope_tail(qn[:, :n], Bqn, qb[:, :n], Bqb, perm_gq, *tabs, dst, Bdst, t0, t1)

                for j in range(8):
                    wb, Bwb = load_w(j)
                    dst, Bdst = (QTda, BQTda) if j < 4 else (KTda, BKTda)
                    for (t0, t1) in BLK:
                        pp, Bpp = proj_T(wb, Bwb, t0, t1)
                        run_block(post_da(pp, Bpp, dst[j % 4], Bdst, t0, t1))
                for j in list(range(16, 26)):
                    wb, Bwb = load_w(j)
                    isq = j < 24
                    dst, Bdst = (QTgq[j - 16], BQTgq) if isq else (KTgq[j - 24], BKTgq)
                    gc = gcol[:, 0:1] if isq else gcol[:, 1:2]
                    for (t0, t1) in BLK:
                        pp, Bpp = proj_T(wb, Bwb, t0, t1)
                        run_block(post_gq(pp, Bpp, gc, dst, Bdst, t0, t1))
                for j, grow in [(12 + i, i * 128) for i in range(4)] + [(28 + i, 512 + i * 128) for i in range(8)] + [(48 + i, 1536 + i * 128) for i in range(4)]:
                    wb, Bwb = load_w(j)
                    for (t0, t1) in BLK:
                        n = t1 - t0
                        pp, Bpp = proj_T(wb, Bwb, t0, t1)
                        advance()
                        of, Bof = ofr.next()
                        P.act(of[:, :n], pp, AF.Silu, [Bpp], [Bof])
                        P.dma(GT[grow:grow + 128, t0:t1], of[:, :n], reads=[Bof], writes=[BGT], disjoint=True)
                flush()
                wvb = s2("wvb", [128, 8, 512], BF16)
                Bwvb = Buf()
                for (c0, nv, dst, Bdst) in ((1024, 512, Vda, BVda), (3328, 256, Vgq, BVgq)):
                    for c4 in range(nv // 128):
                        ws, Bws = wst.next()
                        P.dma(ws, w_in[l][:, c0 + c4 * 128:c0 + (c4 + 1) * 128].rearrange("(k p) c -> p k c", p=128), writes=[Bws])
                        P.cp(wvb[:, :, c4 * 128:(c4 + 1) * 128], ws, [Bws], [Bwvb], eng="pool")
                    for i in range(NT):
                        pb = mmi[0] % 4
                        mmi[0] += 1
                        for k in range(8):
                            P.mm(ps[pb][:, :nv], hT[:, k, i * 128:(i + 1) * 128], wvb[:, k, :nv], k == 0, k == 7, [Bwvb, BhT], [Bps[pb]])
                        ob, Bob = obr.next()
                        P.act(ob[:, :nv], ps[pb][:, :nv], AF.Copy, [Bps[pb]], [Bob])
                        P.dma(dst[i * 128:(i + 1) * 128, :], ob[:, :nv], reads=[Bob], writes=[Bdst], disjoint=True)

                def conv_chunk(jj, outbuf, Boutb):
                    wb, Bwb = load_w(36 + jj)
                    for (t0, t1) in BLK:
                        pp, Bpp = proj_T(wb, Bwb, t0, t1)
                        advance()
                        P.act(uraw[:, t0:t1], pp, AF.Copy, [Bpp], [Buraw], )
                        P.act(outbuf[:, t0:t1], pp, AF.Identity, [Bpp, Bcol, Bcols], [Boutb], scale=swc[:, jj, 1:2], bias=sbc[:, jj:jj + 1])
                    for (s0, s1) in ((0, LC), (LC, T)):
                        P.stt(outbuf[:, s0 + 1:s1], uraw[:, s0:s1 - 1], swc[:, jj, 0:1], outbuf[:, s0 + 1:s1], ALU.mult, ALU.add, [Buraw, Bcol, Boutb], [Boutb])
                        P.stt(outbuf[:, s0:s1 - 1], uraw[:, s0 + 1:s1], swc[:, jj, 2:3], outbuf[:, s0:s1 - 1], ALU.mult, ALU.add, [Buraw, Bcol, Boutb], [Boutb])

                def tr_gen(cc):
                    yield
                    for i0 in range(0, NT, 4):
                        ni = min(4, NT - i0)
                        pb = 6 + ((i0 // 4) % 2)
                        for ii in range(ni):
                            i = i0 + ii
                            P.op("pe", lambda e: e.transpose(out=ps[pb][:, ii * 128:(ii + 1) * 128], in_=x1b[:, i * 128:(i + 1) * 128], identity=identf), [Bx1b, Bconst], [Bps[pb]])
                        ut, But_ = utk.next()
                        P.act(ut[:, :ni, :], ps[pb][:, :ni * 128].rearrange("p (a c) -> p a c", a=ni), AF.Copy, [Bps[pb]], [But_])
                        P.dma(Utok[i0 * 128:(i0 + ni) * 128, cc * 128:(cc + 1) * 128].rearrange("(a p) c -> p a c", p=128), ut[:, :ni, :], reads=[But_], writes=[BUtok], disjoint=True)

                for cc in range(4):
                    conv_chunk(cc, ucv, Bucv)
                    flush()
                    P.dma(X0T[cc * 128:(cc + 1) * 128, :], ucv, reads=[Bucv], writes=[BX0T], disjoint=True)
                    conv_chunk(4 + cc, x1b, Bx1b)
                    conv_chunk(8 + cc, ucv, Bucv)
                    P.tt(x1b, x1b, ucv, ALU.mult, [Bx1b, Bucv], [Bx1b], eng="pool")
                    P.dma(UT[cc * 128:(cc + 1) * 128, :], x1b, reads=[Bx1b], writes=[BUT], disjoint=True)
                    g = tr_gen(cc)
                    next(g)
                    pending.append(g)
                flush()
            P.barrier()
            if STOP == "B":
                raise _Stop

        seqs = [(L, LC, T)] + ([(LC, 0, LC)] if not last else [])
        for (Ls, s0, s1) in seqs:
            ntc = Ls // 128
            NF = 2 * Ls
            with contextlib.ExitStack() as st2:
                def s2(name, shape, dt=F32):
                    return st2.enter_context(nc.sbuf_tensor(_u(name), list(shape), dt)).ap()
                zs = s2("zs", [33, Ls])
                w1s = s2("w1s", [33, 64])
                w2s = s2("w2s", [64, 64])
                w3s = s2("w3s", [64, 64])
                w4s = s2("w4s", [64, 1024])
                bs4 = s2("bs4", [64, 4])
                bs = bs4[:, 0:3]
                fq = bs4[:, 3:4]
                fb = s2("fb", [64, 3])
                Bfw = Buf()
                P.dma(zs, zT[Ls], writes=[Bfw], disjoint=True)
                P.dma(w1s, hy_w1[l], writes=[Bfw], disjoint=True)
                P.dma(w2s, hy_w2[l], writes=[Bfw], disjoint=True)
                P.dma(w3s, hy_w3[l], writes=[Bfw], disjoint=True)
                P.dma(w4s, hy_w4[l], writes=[Bfw], disjoint=True)
                P.dma(bs4, hy_b[l], writes=[Bfw], disjoint=True)
                Bfb = Buf()
                P.ts(fb, bs, fq[:, 0:1], None, ALU.mult, None, [Bfw], [Bfb])
                hbuf = [s2("hf%d" % i, [64, Ls]) for i in range(3)]
                Bhb = [Buf() for _ in range(3)]
                ar = Ring(nc, st2, "far", [64, 512], F32, 2)
                mr = Ring(nc, st2, "fmr", [64, 512], F32, 4)
                nb = min(512, Ls)
                for li in range(3):
                    src = zs if li == 0 else hbuf[li - 1]
                    Bsrc = Bfw if li == 0 else Bhb[li - 1]
                    wl = (w1s, w2s, w3s)[li]
                    for bi, t0 in enumerate(range(0, Ls, nb)):
                        pb = bi % 2
                        P.mm(ps[pb][:64, :nb], wl, src[:, t0:t0 + nb], True, True, [Bfw, Bsrc], [Bps[pb]])
                        a, Ba = ar.next()
                        P.ts(a[:, :nb], ps[pb][:64, :nb], fq[:, 0:1], fb[:, li:li + 1], ALU.mult, ALU.add, [Bps[pb], Bfw, Bfb], [Ba])
                        for (cmp_, thr, add_) in ((ALU.is_gt, math.pi, -2 * math.pi), (ALU.is_lt, -math.pi, 2 * math.pi)):
                            m, Bm_ = mr.next()
                            P.ts(m[:, :nb], a[:, :nb], thr, add_, cmp_, ALU.mult, [Ba], [Bm_])
                            m2, Bm2 = mr.next()
                            P.tt(m2[:, :nb], a[:, :nb], m[:, :nb], ALU.add, [Ba, Bm_], [Bm2])
                            a, Ba = m2, Bm2
                        P.act(hbuf[li][:, t0:t0 + nb], a[:, :nb], AF.Sin, [Ba], [Bhb[li]])
                dr = Ring(nc, st2, "fdr", [128, 512], F32, 2)
                fr = Ring(nc, st2, "ffr", [128, 1024], F32, 2)
                sqr = Ring(nc, st2, "fsq", [128, 1024], F32, 2)
                fbr = Ring(nc, st2, "ffb", [128, 1024], BF16, 2)
                for tc in range(ntc):
                    dc, Bdc = dr.next()
                    P.dma(dc, decay[Ls][tc * 128:(tc + 1) * 128, :], writes=[Bdc])
                    f, Bf = fr.next()
                    for hf in range(2):
                        pb = hf
                        P.mm(ps[pb], hbuf[2][:, tc * 128:(tc + 1) * 128], w4s[:, hf * 512:(hf + 1) * 512], True, True, [Bhb[2], Bfw], [Bps[pb]])
                        P.tt(f[:, hf * 512:(hf + 1) * 512], ps[pb], dc, ALU.mult, [Bps[pb], Bdc], [Bf])
                    if tc == 0:
                        P.op("dve", lambda e: e.memset(f[0:1, 512:1024], 0.0), [Bf], [Bf])
                    sq, Bsq = sqr.next()
                    P.act(sq, f, AF.Square, [Bf], [Bsq])
                    for hf in range(2):
                        P.mm(ps[2], ones_f, sq[:, hf * 512:(hf + 1) * 512], tc == 0 and hf == 0, tc == ntc - 1 and hf == 1, [Bconst, Bsq], [Bps[2]])
                    fbt, Bfbt = fbr.next()
                    P.cp(fbt, f, [Bf], [Bfbt], eng="pool")
                    P.dma(Ftok[tc * 128:(tc + 1) * 128, :], fbt, reads=[Bfbt], writes=[BFtok], disjoint=True)
                rn1 = s2("rn1", [128, 512])
                rn2 = s2("rn2", [128, 512])
                Brn1, Brn2 = Buf(), Buf()
                rstd_from(rn2, ps[2], 1.0, [Bps[2]], [Brn2], rn1, Brn1)
                P.dma(RN, rn2, reads=[Brn2], writes=[BRN])
            P.barrier()
            if STOP == "F":
                raise _Stop

            with contextlib.ExitStack() as st2:
                def s2(name, shape, dt=F32):
                    return st2.enter_context(nc.sbuf_tensor(_u(name), list(shape), dt)).ap()
                ud = s2("ud", [128, ntc, 512], BF16)
                fd = s2("fd", [128, ntc, 1024], BF16)
                rn = s2("rn", [128, 512])
                Bd = Buf()
                P.dma(ud, Utok[s0:s1, :].rearrange("(a p) c -> p a c", p=128), reads=[BUtok], writes=[Bd], disjoint=True)
                P.dma(fd, Ftok[0:Ls, :].rearrange("(a p) c -> p a c", p=128), reads=[BFtok], writes=[Bd], disjoint=True)
                P.dma(rn, RN, reads=[BRN], writes=[Bd], disjoint=True)
                cr = Ring(nc, st2, "dcr", [128, Ls], BF16, 2)
                srr = Ring(nc, st2, "dsr", [128, Ls], BF16, 2)
                e = [Ring(nc, st2, "de%d" % i, [128, 512], F32, 2) for i in range(10)]
                yb = [Ring(nc, st2, "dy%d" % i, [128, 512], BF16, 2) for i in range(2)]
                dpre = {}

                def dload(kc_):
                    cm_, Bcm_ = cr.next()
                    sm_, Bsm_ = srr.next()
                    P.dma(cm_, Cf[Ls][kc_], writes=[Bcm_])
                    P.dma(sm_, Sf[Ls][kc_], writes=[Bsm_])
                    dpre[kc_] = (cm_, Bcm_, sm_, Bsm_)

                dload(0)
                for kc in range(ntc):
                    if kc + 1 < ntc:
                        dload(kc + 1)
                    cm, Bcm, sm, Bsm = dpre.pop(kc)
                    base = (6 * kc) % 8
                    bk = [(base + i_) % 8 for i_ in range(6)]
                    for mi, (mat, Bmat) in enumerate(((cm, Bcm), (sm, Bsm))):
                        for di in range(3):
                            pb = bk[mi * 3 + di]
                            for tc in range(ntc):
                                data = ud[:, tc, :] if di == 0 else fd[:, tc, (di - 1) * 512:di * 512]
                                P.mm(ps[pb], mat[:, tc * 128:(tc + 1) * 128], data, tc == 0, tc == ntc - 1, [Bmat, Bd], [Bps[pb]])
                    ure, Bure = e[0].next()
                    uim, Buim = e[1].next()
                    bre, Bbre = e[2].next()
                    bim, Bbim = e[3].next()
                    P.act(ure, ps[bk[0]], AF.Copy, [Bps[bk[0]]], [Bure])
                    P.act(bre, ps[bk[2]], AF.Copy, [Bps[bk[2]]], [Bbre])
                    hre, Bhre = e[4].next()
                    him, Bhim = e[5].next()
                    P.tt(hre, ps[bk[1]], bre, ALU.add, [Bps[bk[1]], Bbre], [Bhre])
                    P.tt(hre, hre, rn, ALU.mult, [Bhre, Bd], [Bhre])
                    P.act(uim, ps[bk[3]], AF.Copy, [Bps[bk[3]]], [Buim])
                    P.act(bim, ps[bk[5]], AF.Copy, [Bps[bk[5]]], [Bbim])
                    P.tt(him, ps[bk[4]], bim, ALU.subtract, [Bps[bk[4]], Bbim], [Bhim])
                    if kc == 0:
                        P.tt(him[0:1, :], ps[bk[4]][0:1, :], bim[0:1, :], ALU.add, [Bps[bk[4]], Bbim, Bhim], [Bhim])
                    P.tt(him, him, rn, ALU.mult, [Bhim, Bd], [Bhim])
                    t1, Bt1 = e[6].next()
                    t2, Bt2 = e[7].next()
                    t3, Bt3 = e[8].next()
                    t4, Bt4 = e[9].next()
                    P.tt(t1, ure, hre, ALU.mult, [Bure, Bhre], [Bt1], eng="pool")
                    P.tt(t2, uim, him, ALU.mult, [Buim, Bhim], [Bt2], eng="pool")
                    P.tt(t3, ure, him, ALU.mult, [Bure, Bhim], [Bt3], eng="pool")
                    P.tt(t4, uim, hre, ALU.mult, [Buim, Bhre], [Bt4], eng="pool")
                    yre, Byre = yb[0].next()
                    yim, Byim = yb[1].next()
                    P.tt(yre, t1, t2, ALU.subtract, [Bt1, Bt2], [Byre])
                    P.tt(yim, t3, t4, ALU.add, [Bt3, Bt4], [Byim])
                    if kc == 0:
                        P.ts(yre[0:1, :], t1[0:1, :], 0.5, None, ALU.mult, None, [Bt1, Byre], [Byre])
                        P.ts(yim[0:1, :], t2[0:1, :], 0.5, None, ALU.mult, None, [Bt2, Byim], [Byim])
                    P.dma(Yf[0, kc * 128:(kc + 1) * 128, :], yre, reads=[Byre], writes=[BYf], disjoint=True)
                    P.dma(Yf[1, kc * 128:(kc + 1) * 128, :], yim, reads=[Byim], writes=[BYf], disjoint=True)
            P.barrier()
            if STOP == "D":
                raise _Stop

            with contextlib.ExitStack() as st2:
                def s2(name, shape, dt=F32):
                    return st2.enter_context(nc.sbuf_tensor(_u(name), list(shape), dt)).ap()
                yy = s2("yy", [128, 2, ntc, 512], BF16)
                By = Buf()
                for ri in range(2):
                    P.dma(yy[:, ri], Yf[ri, 0:Ls, :].rearrange("(a p) c -> p a c", p=128), reads=[BYf], writes=[By], disjoint=True)
                skc = colsb[:, 11:15]
                ogc = colsb[:, 15:19]
                nh = 2 if ntc >= 16 else 1
                hk = ntc // nh
                nbk = min(512, Ls)
                cir = Ring(nc, st2, "icr", [128, hk, nbk], BF16, 2)
                sir = Ring(nc, st2, "isr", [128, hk, nbk], BF16, 2)
                lu = Ring(nc, st2, "ilu", [128, 512], F32, 2)
                lx = Ring(nc, st2, "ilx", [128, 512], F32, 2)
                lg = Ring(nc, st2, "ilg", [128, 512], F32, 4)
                tq = Ring(nc, st2, "itq", [128, 512], F32, 2)
                ocr = Ring(nc, st2, "ioc", [128, 512], F32, 8)
                ob2 = Ring(nc, st2, "iob", [128, 512], BF16, 3)
                r1r = Ring(nc, st2, "ir1", [128, 512], F32, 2)
                sqi = Ring(nc, st2, "isq", [128, 512], BF16, 8)
                ipend = []

                def iadvance():
                    for g in list(ipend):
                        try:
                            next(g)
                        except StopIteration:
                            ipend.remove(g)

                def epi(tb, bset):
                    g0 = s0 + tb
                    ocs = []
                    sqs = []
                    for cc in range(4):
                        u_, Bu_ = lu.next()
                        x_, Bx_ = lx.next()
                        P.dma(u_[:, :nbk], UT[cc * 128:(cc + 1) * 128, g0:g0 + nbk], reads=[BUT], writes=[Bu_])
                        P.dma(x_[:, :nbk], X0T[cc * 128:(cc + 1) * 128, g0:g0 + nbk], reads=[BX0T], writes=[Bx_])
                        t_, Bt_ = tq.next()
                        P.ts(t_[:, :nbk], u_[:, :nbk], skc[:, cc:cc + 1], None, ALU.mult, None, [Bu_, Bcols], [Bt_], eng="pool")
                        oc, Boc = ocr.next()
                        P.stt(oc[:, :nbk], ps[bset[cc]][:, :nbk], 2.0 / NF, t_[:, :nbk], ALU.mult, ALU.add, [Bps[bset[cc]], Bt_], [Boc])
                        P.tt(oc[:, :nbk], oc[:, :nbk], x_[:, :nbk], ALU.mult, [Boc, Bx_], [Boc], eng="pool")
                        sq, Bsq = sqi.next()
                        P.act(sq[:, :nbk], oc[:, :nbk], AF.Square, [Boc], [Bsq])
                        ocs.append((oc, Boc))
                        sqs.append((sq, Bsq))
                    yield
                    pq = bset[0]
                    for cc in range(4):
                        sq, Bsq = sqs[cc]
                        P.mm(ps[pq][:, :nbk], ones_b, sq[:, :nbk], cc == 0, cc == 3, [Bconst, Bsq], [Bps[pq]])
                    r1, Br1 = r1r.next()
                    rs, Brs = r1r.next()
                    rstd_from(rs[:, :nbk], ps[pq][:, :nbk], 1.0 / 512, [Bps[pq]], [Brs], r1[:, :nbk], Br1)
                    for cc in range(4):
                        oc, Boc = ocs[cc]
                        g_, Bg_ = lg.next()
                        P.dma(g_[:, :nbk], GT[1536 + cc * 128:1536 + (cc + 1) * 128, g0:g0 + nbk], reads=[BGT], writes=[Bg_])
                        P.stt(oc[:, :nbk], oc[:, :nbk], ogc[:, cc:cc + 1], rs[:, :nbk], ALU.mult, ALU.mult, [Boc, Bcols, Brs], [Boc])
                        ob, Bob = ob2.next()
                        P.tt(ob[:, :nbk], oc[:, :nbk], g_[:, :nbk], ALU.mult, [Boc, Bg_], [Bob], eng="pool")
                        P.dma(YMT[1536 + cc * 128:1536 + (cc + 1) * 128, g0:g0 + nbk], ob[:, :nbk], reads=[Bob], writes=[BYMT], disjoint=True)

                ipre = {}

                def iload(tb_, h_):
                    cm_, Bcm_ = cir.next()
                    sm_, Bsm_ = sir.next()
                    P.dma(cm_, Ci[Ls][:, h_ * hk:(h_ + 1) * hk, tb_:tb_ + nbk], writes=[Bcm_])
                    P.dma(sm_, Si[Ls][:, h_ * hk:(h_ + 1) * hk, tb_:tb_ + nbk], writes=[Bsm_])
                    ipre[(tb_, h_)] = (cm_, Bcm_, sm_, Bsm_)

                for tbi, tb in enumerate(range(0, Ls, nbk)):
                    bset = (0, 1, 2, 3) if tbi % 2 == 0 else (4, 5, 6, 7)
                    for h in range(nh):
                        if (tb, h) not in ipre:
                            iload(tb, h)
                        nxt = (tb, h + 1) if h + 1 < nh else (tb + nbk, 0)
                        if nxt[0] < Ls:
                            iload(*nxt)
                        cm, Bcm, sm, Bsm = ipre.pop((tb, h))
                        for cc in range(4):
                            pc = bset[cc]
                            for kk in range(hk):
                                kc = h * hk + kk
                                P.mm(ps[pc][:, :nbk], yy[:, 0, kc, cc * 128:(cc + 1) * 128], cm[:, kk, :], kc == 0, False, [By, Bcm], [Bps[pc]])
                                P.mm(ps[pc][:, :nbk], yy[:, 1, kc, cc * 128:(cc + 1) * 128], sm[:, kk, :], False, kc == ntc - 1, [By, Bsm], [Bps[pc]])
                        if h == 0:
                            iadvance()
                    g = epi(tb, bset)
                    next(g)
                    ipend.append(g)
                while ipend:
                    iadvance()
            P.barrier()
            if STOP == "I":
                raise _Stop

        qsets = [(BLK[1:], 0, T)] + ([([BLK[0]], 0, LC)] if not last else [])
        for which in ("da", "gq"):
            with contextlib.ExitStack() as st2:
                def s2(name, shape, dt=F32):
                    return st2.enter_context(nc.sbuf_tensor(_u(name), list(shape), dt)).ap()
                nkv = 4 if which == "da" else 2
                ncomp = 2 if which == "da" else 1
                nheads = 4 if which == "da" else 8
                KTs, BKTs = (KTda, BKTda) if which == "da" else (KTgq, BKTgq)
                Vs, BVs = (Vda, BVda) if which == "da" else (Vgq, BVgq)
                QTs, BQTs = (QTda, BQTda) if which == "da" else (QTgq, BQTgq)
                scale = 0.125 if which == "da" else 128 ** -0.5
                kt = s2("kt", [128, nkv, T], BF16)
                vt = s2("vt", [128, NT, nkv * 128], BF16)
                Bkv = Buf()
                for h in range(nkv):
                    P.dma(kt[:, h, :], KTs[h], reads=[BKTs], writes=[Bkv], disjoint=True)
                P.dma(vt, Vs.rearrange("(a p) c -> p a c", p=128), reads=[BVs], writes=[Bkv], disjoint=True)
                gcol2 = s2("gcol2", [128, 8])
                Bg2 = Buf()
                if which == "da":
                    P.ts(gcol2[:, 0:1], colsb[:, 2:3], 1.0 - lam_init, None, ALU.mult, None, [Bcols], [Bg2])
                else:
                    P.cp(gcol2, colsb[:, 3:11], [Bcols], [Bg2])
                qr = Ring(nc, st2, "aq", [128, 512], BF16, 3)
                qr1 = Ring(nc, st2, "aq1", [128, 512], BF16, 3)
                if which == "da":
                    for rr in (qr, qr1):
                        for tq_, bq_ in zip(rr.t, rr.b):
                            P.op("dve", lambda e: e.memset(tq_, 0.0), writes=[bq_])
                pr = Ring(nc, st2, "ap", [128, 512], BF16, 5)
                evr = Ring(nc, st2, "aev", [128, 512], F32, 4)
                rcr = Ring(nc, st2, "arc", [128, 512], F32, 4)
                ohr = Ring(nc, st2, "aoh", [128, 512], F32, 10)
                sqr2 = Ring(nc, st2, "asq", [128, 512], BF16, 2)
                r1r = Ring(nc, st2, "ar1", [128, 512], F32, 2)
                glr = Ring(nc, st2, "agl", [128, 512], F32, 3)
                obr2 = Ring(nc, st2, "aob", [128, 512], BF16, 3)
                steps = []
                ja = 0
                for (qblocks, k0, k1) in qsets:
                    for (t0, t1) in qblocks:
                        blk = {"heads_o": [], "q": {}, "comps": {}}
                        for h in range(nheads):
                            for c in range(ncomp):
                                job = dict(t0=t0, t1=t1, n=t1 - t0, h=h, c=c, k0=k0, nk=(k1 - k0) // 128, blk=blk,
                                           kvh=(h if which == "da" else h // 4), pO=2 + (ja % 2) * 2)
                                ja += 1
                                for ki in range(job["nk"]):
                                    steps.append((job, ki))

                def emit_qk(si_, job, ki):
                    n, h, c, t0, t1 = job["n"], job["h"], job["c"], job["t0"], job["t1"]
                    blk = job["blk"]
                    if h not in blk["q"]:
                        q_, Bq_ = qr.next()
                        if which == "da":
                            q1_, Bq1_ = qr1.next()
                            P.dma(q_[0:64, :n], QTs[h][0:64, t0:t1], reads=[BQTs], writes=[Bq_])
                            P.dma(q1_[64:128, :n], QTs[h][64:128, t0:t1], reads=[BQTs], writes=[Bq1_])
                            blk["q"][h] = ((q_, Bq_), (q1_, Bq1_))
                        else:
                            P.dma(q_[:, :n], QTs[h][:, t0:t1], reads=[BQTs], writes=[Bq_])
                            blk["q"][h] = ((q_, Bq_),)
                    q_, Bq_ = blk["q"][h][c]
                    kk0 = job["k0"] + ki * 128
                    pS = SB[si_ % 3]
                    P.mm(ps[pS][:, :n], kt[:, job["kvh"], kk0:kk0 + 128], q_[:, :n], True, True, [Bkv, Bq_], [Bps[pS]])

                def emit_rest(si_, job, ki):
                    n, h, c, t0, t1, nk, kvh = job["n"], job["h"], job["c"], job["t0"], job["t1"], job["nk"], job["kvh"]
                    blk = job["blk"]
                    pO = job["pO"]
                    pD = pO + 1
                    pS = SB[si_ % 3]
                    kk0 = job["k0"] + ki * 128
                    p_, Bp_ = pr.next()
                    P.act(p_[:, :n], ps[pS][:, :n], AF.Exp, [Bps[pS]], [Bp_], scale=scale)
                    P.mm(ps[pO][:, :n], vt[:, kk0 // 128, kvh * 128:(kvh + 1) * 128], p_[:, :n], ki == 0, ki == nk - 1, [Bkv, Bp_], [Bps[pO]])
                    P.mm(ps[pD][:, :n], ones_b, p_[:, :n], ki == 0, ki == nk - 1, [Bconst, Bp_], [Bps[pD]])
                    if ki != nk - 1:
                        return
                    rc, Brc = rcr.next()
                    P.op("dve", lambda e: e.reciprocal(out=rc[:, :n], in_=ps[pD][:, :n]), [Bps[pD]], [Brc])
                    ev, Bev = evr.next()
                    P.tt(ev[:, :n], ps[pO][:, :n], rc[:, :n], ALU.mult, [Bps[pO], Brc], [Bev])
                    blk["comps"][(h, c)] = (ev, Bev)
                    if c != ncomp - 1:
                        return
                    oh, Boh = ohr.next()
                    if which == "da":
                        e0, Be0 = blk["comps"][(h, 0)]
                        e1, Be1 = blk["comps"][(h, 1)]
                        P.stt(oh[:, :n], e1[:, :n], lam_c[:, 0:1], e0[:, :n], ALU.mult, ALU.add, [Be1, Be0, Blam], [Boh])
                    else:
                        e0, Be0 = blk["comps"][(h, 0)]
                        P.cp(oh[:, :n], e0[:, :n], [Be0], [Boh], eng="pool")
                    sq, Bsq = sqr2.next()
                    P.act(sq[:, :n], oh[:, :n], AF.Square, [Boh], [Bsq])
                    if which == "da":
                        P.mm(ps[6][:, :n], ones_b, sq[:, :n], True, True, [Bconst, Bsq], [Bps[6]])
                        heads_fin = [(h, oh, Boh)]
                        dn = 128
                    else:
                        P.mm(ps[6][:, :n], ones_b, sq[:, :n], h == 0, h == 7, [Bconst, Bsq], [Bps[6]])
                        blk["heads_o"].append((h, oh, Boh))
                        heads_fin = blk["heads_o"] if h == 7 else []
                        dn = 1024
                    if heads_fin:
                        r1, Br1 = r1r.next()
                        rs, Brs = r1r.next()
                        rstd_from(rs[:, :n], ps[6][:, :n], 1.0 / dn, [Bps[6]], [Brs], r1[:, :n], Br1)
                        for (hh, o_, Bo_) in heads_fin:
                            grow = hh * 128 if which == "da" else 512 + hh * 128
                            gcc = gcol2[:, 0:1] if which == "da" else gcol2[:, hh:hh + 1]
                            g_, Bg_ = glr.next()
                            P.dma(g_[:, :n], GT[grow:grow + 128, t0:t1], reads=[BGT], writes=[Bg_])
                            P.stt(o_[:, :n], o_[:, :n], gcc, rs[:, :n], ALU.mult, ALU.mult, [Bo_, Bg2, Brs], [Bo_])
                            ob, Bob = obr2.next()
                            P.tt(ob[:, :n], o_[:, :n], g_[:, :n], ALU.mult, [Bo_, Bg_], [Bob], eng="pool")
                            P.dma(YMT[grow:grow + 128, t0:t1], ob[:, :n], reads=[Bob], writes=[BYMT], disjoint=True)

                SB = (0, 1, 7)
                emit_qk(0, *steps[0])
                if len(steps) > 1:
                    emit_qk(1, *steps[1])
                for si_ in range(len(steps)):
                    if si_ + 2 < len(steps):
                        emit_qk(si_ + 2, *steps[si_ + 2])
                    emit_rest(si_, *steps[si_])
            P.barrier()
            if STOP == which:
                raise _Stop

        with contextlib.ExitStack() as st2:
            def s2(name, shape, dt=F32):
                return st2.enter_context(nc.sbuf_tensor(_u(name), list(shape), dt)).ap()
            wos = s2("wos", [128, 16, 512], F32)
            wob = s2("wob", [128, 16, D], BF16)
            Bwos, Bwob = Buf(), Buf()
            for hf in range(2):
                P.dma(wos, w_out[l][:, hf * 512:(hf + 1) * 512].rearrange("(k p) c -> p k c", p=128), reads=[Bwob], writes=[Bwos])
                P.cp(wob[:, :, hf * 512:(hf + 1) * 512], wos, [Bwos], [Bwob], eng="pool")
            fgr = s2("fgr", [128, D])
            Bfg = Buf()
            P.dma(fgr, final_g.partition_broadcast(128), writes=[Bfg])
            ymr = Ring(nc, st2, "oym", [128, 16, 128], BF16, 3)
            xr = Ring(nc, st2, "ox", [128, D], F32, 3)
            yr = Ring(nc, st2, "oy", [128, D], F32, 3)
            jr = Ring(nc, st2, "oj", [128, D], F32, 1)
            sr = Ring(nc, st2, "oss", [128, 4], F32, 2)
            tiles = range(2, NT) if last else range(NT)
            opre = {}

            def oload(i_):
                ym_, Bym_ = ymr.next()
                P.dma(ym_, YMT[:, i_ * 128:(i_ + 1) * 128].rearrange("(k p) t -> p k t", p=128), reads=[BYMT], writes=[Bym_])
                xt_, Bx_ = xr.next()
                P.dma(xt_, xsrc[i_ * 128:(i_ + 1) * 128, :], reads=[Bxs], writes=[Bx_])
                opre[i_] = (ym_, Bym_, xt_, Bx_)

            tiles = list(tiles)
            oload(tiles[0])
            for ti_, i in enumerate(tiles):
                if ti_ + 1 < len(tiles):
                    oload(tiles[ti_ + 1])
                ym, Bym, xt, Bx = opre.pop(i)
                y_, By_ = yr.next()
                gt = gate_c if i < 2 else gate
                for hf in range(2):
                    pb = (i % 2) * 2 + hf
                    for k in range(16):
                        P.mm(ps[pb], ym[:, k, :], wob[:, k, hf * 512:(hf + 1) * 512], k == 0, k == 15, [Bym, Bwob], [Bps[pb]])
                    P.tt(y_[:, hf * 512:(hf + 1) * 512], ps[pb], gt[:, hf * 512:(hf + 1) * 512], ALU.mult, [Bps[pb], Bgate], [By_])
                P.tt(y_, y_, xt, ALU.add, [By_, Bx], [By_], eng="pool")
                if not last:
                    P.dma(xs[i * 128:(i + 1) * 128, :], y_, reads=[By_], writes=[Bxs], disjoint=True)
                else:
                    jk, Bj = jr.next()
                    ss, Bs = sr.next()
                    P.act(jk, y_, AF.Square, [By_], [Bj, Bs], accum_out=ss[:, 0:1])
                    rstd_from(ss[:, 2:3], ss[:, 0:1], 1.0 / D, [Bs], [Bs], ss[:, 1:2], Bs)
                    P.stt(y_, y_, ss[:, 2:3], fgr, ALU.mult, ALU.mult, [By_, Bs, Bfg], [By_])
                    P.dma(out[(i - 2) * 128:(i - 1) * 128, :], y_, reads=[By_], writes=[Bout], disjoint=True)
        P.barrier()
      except _Stop:
        break
    P.barrier()
    return nc


def _colmajor(v):
    return np.ascontiguousarray(v.reshape(-1, 128).T)


def _consts():
    bf = ml_dtypes.bfloat16
    c = {}
    c["ident"] = np.eye(128, dtype=np.float32).astype(bf)
    c["identf"] = np.eye(128, dtype=np.float32)
    p = np.arange(128)
    pd = np.zeros((128, 128), np.float32)
    part = np.where((p % 32) < 16, p + 16, p - 16)
    pd[part, p] = 1.0
    c["perm_da"] = pd.astype(bf)
    pg = np.zeros((128, 128), np.float32)
    partg = np.where((p % 64) < 32, p + 32, p - 32)
    pg[partg, p] = 1.0
    c["perm_gq"] = pg.astype(bf)
    tl = np.arange(L)
    row = (tl // 64).astype(np.float64)
    col = (tl % 64).astype(np.float64)

    def tables(head_dim, npart_rep):
        axis = head_dim // 2
        inv = 10000.0 ** (-np.arange(0, axis, 2, dtype=np.float64) / axis)
        nf = axis // 2
        cosT = np.ones((head_dim, T), np.float64)
        sinT = np.zeros((head_dim, T), np.float64)
        for d in range(head_dim):
            half = d // axis
            j = d % nf
            isx2 = (d % axis) // nf
            pos = row if half == 0 else col
            ang = pos * inv[j]
            cosT[d, LC:] = np.cos(ang)
            sinT[d, LC:] = np.sin(ang) * (1.0 if isx2 else -1.0)
        cosT = np.tile(cosT, (npart_rep, 1))
        sinT = np.tile(sinT, (npart_rep, 1))
        return cosT.astype(np.float32), sinT.astype(np.float32)

    c["cos_da"], c["sin_da"] = tables(64, 2)
    c["cos_gq"], c["sin_gq"] = tables(128, 1)
    for (Ls, sfx) in ((L, "l"), (LC, "c")):
        t = np.linspace(0.0, 1.0, Ls, dtype=np.float32).astype(np.float64)
        w = 2.0 * math.pi * np.arange(Ls, dtype=np.float64) / Ls
        f = np.linspace(1e-4, 15.0, 16, dtype=np.float32).astype(np.float64)
        z = np.concatenate([t[:, None], np.cos(f[None] * w[:, None]), -np.sin(f[None] * w[:, None])], axis=1)
        c["zT_" + sfx] = np.ascontiguousarray(z.T).astype(np.float32)
        max_decay = math.log(1e-2) / 0.3
        min_decay = math.log(1e-2) / 1.5
        deltas = np.linspace(min_decay, max_decay, 512, dtype=np.float32).astype(np.float64)
        c["decay_" + sfx] = np.exp(-t[:, None] * np.abs(deltas)[None]).astype(np.float32)
        N = 2 * Ls
        tt = np.arange(Ls, dtype=np.int64)
        prod = (tt[:, None] * tt[None, :]) % N
        ang = 2.0 * math.pi * prod.astype(np.float64) / N
        C = np.cos(ang).astype(np.float32)
        S = (-np.sin(ang)).astype(np.float32)
        alt = np.where(tt % 2 == 0, 1.0, -1.0).astype(np.float32)
        Sfm = S.copy()
        Sfm[:, 0] = alt
        Sim = Sfm.T.copy()
        ntc = Ls // 128
        def fwd_arr(M):
            a = M.reshape(ntc, 128, ntc, 128)
            return np.ascontiguousarray(a.transpose(2, 1, 0, 3).reshape(ntc, 128, Ls)).astype(bf)
        def inv_arr(M):
            a = M.reshape(ntc, 128, Ls)
            return np.ascontiguousarray(a.transpose(1, 0, 2)).astype(bf)
        c["cf_" + sfx] = fwd_arr(C)
        c["sf_" + sfx] = fwd_arr(Sfm)
        c["ci_" + sfx] = inv_arr(C)
        c["si_" + sfx] = inv_arr(Sim)
    return c


def kernel(x, c, ctx, c_ctx, ada_w, ada_b, norm_g, w_in, w_out, da_lambda, da_subln_g,
           gq_q_g, gq_k_g, gq_out_g, hy_short_w, hy_short_b, hy_w1, hy_b1, hy_w2, hy_b2,
           hy_w3, hy_b3, hy_w4, hy_freq, hy_bias, hy_out_g, final_g):
    f = lambda a: np.ascontiguousarray(np.asarray(a, dtype=np.float32))
    x, c, ctx, c_ctx = f(x), f(c), f(ctx), f(c_ctx)
    shared = dict(
        cc_t=_colmajor(c_ctx),
        ada_w=f(ada_w), ada_b=f(ada_b), norm_g=f(norm_g), w_in=f(w_in), w_out=f(w_out),
        lamv=f(da_lambda),
        cols=np.stack([np.concatenate([
            f(gq_q_g)[l].reshape(128, 1), f(gq_k_g)[l].reshape(128, 1), f(da_subln_g)[l].reshape(128, 1),
            _colmajor(f(gq_out_g)[l]), _colmajor(f(hy_bias)[l]), _colmajor(f(hy_out_g)[l]),
            _colmajor(f(hy_short_b)[l]), np.zeros((128, 1), np.float32)], axis=1) for l in range(DEPTH)]),
        hy_sw=np.stack([np.ascontiguousarray(f(hy_short_w)[l].T.reshape(12, 128, 3).transpose(1, 0, 2)) for l in range(DEPTH)]),
        hy_w1=f(hy_w1), hy_w2=f(hy_w2), hy_w3=f(hy_w3), hy_w4=f(hy_w4),
        hy_b=np.ascontiguousarray(np.stack([f(hy_b1), f(hy_b2), f(hy_b3), f(hy_freq)], axis=-1)),
        final_g=f(final_g),
    )
    shared.update(_consts())
    in_maps = []
    for b in range(NCORES):
        m = dict(shared)
        m["xin"] = np.ascontiguousarray(np.concatenate([ctx[b], x[b]], axis=0))
        m["c_t"] = _colmajor(c[b])
        in_maps.append(m)
    nc = build(NLAYERS)
    res = run_bass_kernel_spmd(nc, in_maps, core_ids=list(range(NCORES)))
    global _last_results
    _last_results = res.results
    return np.stack([np.asarray(r["out"], dtype=np.float32) for r in res.results], axis=0)
```

```python
import math
import contextlib
import numpy as np
import ml_dtypes
import concourse.bass as bass
import concourse.mybir as mybir
from concourse.bass_utils import run_bass_kernel_spmd

F32 = mybir.dt.float32
BF16 = mybir.dt.bfloat16
AF = mybir.ActivationFunctionType
ALU = mybir.AluOpType

D = 1024
L = 4096
LC = 256
T = L + LC
NT = T // 128
DEPTH = 2
DIN = 6656
DMIX = 2048
EPS = 1e-6
BLK = [(0, 256)] + [(256 + 512 * i, 768 + 512 * i) for i in range(8)]
DEBUG = []
NLAYERS = DEPTH
STOP = None
NCORES = 8
NJ = 8
NB = 9
NTA = NT
SUB = 9
VAR = 0


class _Stop(Exception):
    pass


class Buf:
    __slots__ = ("w", "r", "x")

    def __init__(self, x=False):
        self.w = {}
        self.r = {}
        self.x = x


class Prog:
    def __init__(self, nc, n_dma_sems=32):
        self.nc = nc
        self.eng = {"pe": nc.tensor, "act": nc.scalar, "dve": nc.vector, "pool": nc.gpsimd, "sp": nc.sync}
        self.sem = {e: nc.alloc_semaphore("s_" + e) for e in self.eng}
        self.cnt = {e: 0 for e in self.eng}
        self.seen = {e: {} for e in self.eng}
        self.dsem = [nc.alloc_semaphore("d%d" % i) for i in range(n_dma_sems)]
        self.dcnt = [0] * n_dma_sems
        self.dnext = 0
        self.semobj = {}
        for e in self.eng:
            self.semobj[("e", e)] = self.sem[e]
        for i, s in enumerate(self.dsem):
            self.semobj[("d", i)] = s

    def _wait(self, e, key, val):
        if val <= 0:
            return
        seen = self.seen[e]
        if seen.get(key, 0) >= val:
            return
        seen[key] = val
        self.eng[e].wait_ge(self.semobj[key], val)

    def _deps(self, e, reads, writes, disjoint):
        deps = {}
        for b in reads:
            for k, v in b.w.items():
                if deps.get(k, 0) < v:
                    deps[k] = v
            if b.x:
                for k, v in b.r.items():
                    if k != ("e", e) and deps.get(k, 0) < v:
                        deps[k] = v
        for b in writes:
            if not disjoint:
                for k, v in b.w.items():
                    if deps.get(k, 0) < v:
                        deps[k] = v
            for k, v in b.r.items():
                if k == ("e", e):
                    continue
                if deps.get(k, 0) < v:
                    deps[k] = v
        if e == "pe":
            deps.pop(("e", "pe"), None)
        for k, v in deps.items():
            self._wait(e, k, v)

    def _mark(self, key, val, reads, writes, disjoint):
        for b in reads:
            if b.r.get(key, 0) < val:
                b.r[key] = val
        for b in writes:
            if not disjoint:
                b.w = {key: val}
            else:
                if b.w.get(key, 0) < val:
                    b.w[key] = val
            b.r = {}

    def op(self, e, fn, reads=(), writes=(), disjoint=False):
        self._deps(e, reads, writes, disjoint)
        ins = fn(self.eng[e])
        self.cnt[e] += 1
        ins.then_inc(self.sem[e], 1)
        self._mark(("e", e), self.cnt[e], reads, writes, disjoint)
        return ins

    def dma(self, out, in_, reads=(), writes=(), q="sp", disjoint=False, **kw):
        self._deps(q, reads, writes, disjoint)
        i = self.dnext
        self.dnext = (self.dnext + 1) % len(self.dsem)
        key = ("d", i)
        self._wait(q, key, self.dcnt[i])
        ins = self.eng[q].dma_start(out=out, in_=in_, **kw)
        self.dcnt[i] += 16
        ins.then_inc(self.dsem[i], 16)
        self._mark(key, self.dcnt[i], reads, writes, disjoint)
        return ins

    def barrier(self):
        for e in self.eng:
            for f in self.eng:
                if f != e:
                    self._wait(e, ("e", f), self.cnt[f])
            for i in range(len(self.dsem)):
                self._wait(e, ("d", i), self.dcnt[i])

    def act(self, out, in_, func, reads, writes, **kw):
        return self.op("act", lambda e: e.activation(out=out, in_=in_, func=func, **kw), reads, writes)

    def mm(self, out, lhsT, rhs, start, stop, reads, writes):
        return self.op("pe", lambda e: e.matmul(out, lhsT=lhsT, rhs=rhs, start=start, stop=stop), reads, writes)

    def tt(self, out, in0, in1, op, reads, writes, eng="dve"):
        return self.op(eng, lambda e: e.tensor_tensor(out=out, in0=in0, in1=in1, op=op), reads, writes)

    def ts(self, out, in0, s1, s2, op0, op1, reads, writes, eng="dve"):
        if op1 is None:
            return self.op(eng, lambda e: e.tensor_scalar(out=out, in0=in0, scalar1=s1, scalar2=None, op0=op0), reads, writes)
        return self.op(eng, lambda e: e.tensor_scalar(out=out, in0=in0, scalar1=s1, scalar2=s2, op0=op0, op1=op1), reads, writes)

    def stt(self, out, in0, scalar, in1, op0, op1, reads, writes):
        return self.op("dve", lambda e: e.scalar_tensor_tensor(out=out, in0=in0, scalar=scalar, in1=in1, op0=op0, op1=op1), reads, writes)

    def cp(self, out, in_, reads, writes, eng="dve"):
        return self.op(eng, lambda e: e.tensor_copy(out=out, in_=in_), reads, writes)


_UID = [0]


def _u(name):
    _UID[0] += 1
    return "%s_%d" % (name, _UID[0])


class Ring:
    def __init__(self, nc, stack, name, shape, dtype, n):
        self.t = [stack.enter_context(nc.sbuf_tensor(_u(name), shape, dtype)).ap() for i in range(n)]
        self.b = [Buf() for _ in range(n)]
        self.i = 0

    def next(self):
        j = self.i
        self.i = (self.i + 1) % len(self.t)
        return self.t[j], self.b[j]


def build(nlayers=DEPTH):
    nc = bass.Bass("TRN2", target_bir_lowering=False)
    P = Prog(nc)

    def din(name, shape, dt=F32):
        return nc.dram_tensor(name, list(shape), dt, kind="ExternalInput").ap()

    def scr(name, shape, dt):
        kind = "ExternalOutput" if name in DEBUG else "Internal"
        return nc.dram_tensor(name, list(shape), dt, kind=kind).ap(), Buf()

    xin = din("xin", [T, D])
    c_t = din("c_t", [128, 8])
    cc_t = din("cc_t", [128, 8])
    ada_w = din("ada_w", [DEPTH, D, 3 * D])
    ada_b = din("ada_b", [DEPTH, 3 * D])
    norm_g = din("norm_g", [DEPTH, D])
    w_in = din("w_in", [DEPTH, D, DIN])
    w_out = din("w_out", [DEPTH, DMIX, D])
    lamv = din("lamv", [DEPTH, 4, 64])
    cols_in = din("cols", [DEPTH, 128, 32])
    hy_sw = din("hy_sw", [DEPTH, 128, 12, 3])
    hy_w1 = din("hy_w1", [DEPTH, 33, 64])
    hy_w2 = din("hy_w2", [DEPTH, 64, 64])
    hy_w3 = din("hy_w3", [DEPTH, 64, 64])
    hy_w4 = din("hy_w4", [DEPTH, 64, 1024])
    hy_b = din("hy_b", [DEPTH, 64, 4])
    final_g = din("final_g", [D])
    ident_in = din("ident", [128, 128], BF16)
    identf_in = din("identf", [128, 128], F32)
    perm_da_in = din("perm_da", [128, 128], BF16)
    perm_gq_in = din("perm_gq", [128, 128], BF16)
    cos_da = din("cos_da", [128, T])
    sin_da = din("sin_da", [128, T])
    cos_gq = din("cos_gq", [128, T])
    sin_gq = din("sin_gq", [128, T])
    zT = {L: din("zT_l", [33, L]), LC: din("zT_c", [33, LC])}
    decay = {L: din("decay_l", [L, 512]), LC: din("decay_c", [LC, 512])}
    Cf = {L: din("cf_l", [L // 128, 128, L], BF16), LC: din("cf_c", [LC // 128, 128, LC], BF16)}
    Sf = {L: din("sf_l", [L // 128, 128, L], BF16), LC: din("sf_c", [LC // 128, 128, LC], BF16)}
    Ci = {L: din("ci_l", [128, L // 128, L], BF16), LC: din("ci_c", [128, LC // 128, LC], BF16)}
    Si = {L: din("si_l", [128, L // 128, L], BF16), LC: din("si_c", [128, LC // 128, LC], BF16)}
    out = nc.dram_tensor("out", [L, D], F32, kind="ExternalOutput").ap()
    Bout = Buf()

    xs, Bxs = scr("xs", [T, D], F32)
    QTda, BQTda = scr("QTda", [4, 128, T], BF16)
    KTda, BKTda = scr("KTda", [4, 128, T], BF16)
    Vda, BVda = scr("Vda", [T, 512], BF16)
    QTgq, BQTgq = scr("QTgq", [8, 128, T], BF16)
    KTgq, BKTgq = scr("KTgq", [2, 128, T], BF16)
    Vgq, BVgq = scr("Vgq", [T, 256], BF16)
    GT, BGT = scr("GT", [DMIX, T], F32)
    X0T, BX0T = scr("X0T", [512, T], F32)
    UT, BUT = scr("UT", [512, T], F32)
    Utok, BUtok = scr("Utok", [T, 512], BF16)
    Ftok, BFtok = scr("Ftok", [L, 1024], BF16)
    RN, BRN = scr("RN", [128, 512], F32)
    Yf, BYf = scr("Yf", [2, L, 512], BF16)
    YMT, BYMT = scr("YMT", [DMIX, T], BF16)

    ps = [nc.alloc_psum_tensor("ps%d" % i, [128, 512], F32).ap() for i in range(8)]
    Bps = [Buf(True) for _ in range(8)]

    def sb(name, shape, dt=F32):
        return nc.alloc_sbuf_tensor(name, list(shape), dt).ap()

    ident = sb("ident_s", [128, 128], BF16)
    identf = sb("identf_s", [128, 128], F32)
    perm_da = sb("perm_da_s", [128, 128], BF16)
    perm_gq = sb("perm_gq_s", [128, 128], BF16)
    ones_f = sb("ones_f", [128, 128], F32)
    ones_b = sb("ones_b", [128, 128], BF16)
    gate = sb("gate_s", [128, D])
    gate_c = sb("gatec_s", [128, D])
    lam_c = sb("lam_c", [128, 2])
    ct_s = sb("ct_s", [128, 8])
    cct_s = sb("cct_s", [128, 8])
    colsb = sb("colsb", [128, 32])
    Bcols = Buf()
    Bconst = Buf()
    Bgate = Buf()
    Blam = Buf()
    P.dma(ident, ident_in, writes=[Bconst], disjoint=True)
    P.dma(identf, identf_in, writes=[Bconst], disjoint=True)
    P.dma(perm_da, perm_da_in, writes=[Bconst], disjoint=True)
    P.dma(perm_gq, perm_gq_in, writes=[Bconst], disjoint=True)
    P.dma(ct_s, c_t, writes=[Bconst], disjoint=True)
    P.dma(cct_s, cc_t, writes=[Bconst], disjoint=True)
    P.op("dve", lambda e: e.memset(ones_f, 1.0), writes=[Bconst], disjoint=True)
    P.op("dve", lambda e: e.memset(ones_b, 1.0), writes=[Bconst], disjoint=True)
    P.barrier()

    def rstd_from(out_ap, in_ap, scale, reads, writes, tmp, Btmp):
        P.act(tmp, in_ap, AF.Ln, reads, [Btmp], scale=scale, bias=epsc[:in_ap.shape[0], :])
        P.act(out_ap, tmp, AF.Exp, [Btmp], writes, scale=-0.5)

    epsc = sb("epsc", [128, 1])
    P.op("dve", lambda e: e.memset(epsc, EPS), writes=[Bconst], disjoint=True)

    for l in range(nlayers):
      try:
        lam_init = 0.8 - 0.6 * math.exp(-0.3 * l)
        last = l == DEPTH - 1
        xsrc = xin if l == 0 else xs
        P.dma(colsb, cols_in[l], writes=[Bcols])

        with contextlib.ExitStack() as st:
            hT = st.enter_context(nc.sbuf_tensor(_u("hT"), [128, 8, T], BF16)).ap()
            BhT = Buf()
            with contextlib.ExitStack() as st2:
                def s2(name, shape, dt=F32):
                    return st2.enter_context(nc.sbuf_tensor(_u(name), list(shape), dt)).ap()
                lm = s2("lm", [128, 4, 64])
                Blm = Buf()
                P.dma(lm, lamv[l].partition_broadcast(128), writes=[Blm])
                lp = s2("lp", [128, 2, 64])
                ls = s2("ls", [128, 2])
                P.tt(lp[:, 0, :], lm[:, 0, :], lm[:, 1, :], ALU.mult, [Blm], [Blam], )
                P.tt(lp[:, 1, :], lm[:, 2, :], lm[:, 3, :], ALU.mult, [Blm, Blam], [Blam])
                P.op("dve", lambda e: e.tensor_reduce(out=ls, in_=lp, op=ALU.add, axis=mybir.AxisListType.X), [Blam], [Blam])
                P.act(ls, ls, AF.Exp, [Blam], [Blam])
                P.tt(lam_c[:, 0:1], ls[:, 1:2], ls[:, 0:1], ALU.subtract, [Blam], [Blam])
                P.ts(lam_c[:, 0:1], lam_c[:, 0:1], -lam_init, None, ALU.add, None, [Blam], [Blam])

                sc = s2("sc", [128, 8])
                scc = s2("scc", [128, 8])
                rep = s2("rep", [128, 8, 128], BF16)
                repc = s2("repc", [128, 8, 128], BF16)
                Bm = Buf()
                P.act(sc, ct_s, AF.Silu, [Bconst], [Bm])
                P.act(scc, cct_s, AF.Silu, [Bconst, Bm], [Bm])
                for k in range(8):
                    P.ts(rep[:, k, :], ones_f, sc[:, k:k + 1], None, ALU.mult, None, [Bm, Bconst], [Bm])
                    P.ts(repc[:, k, :], ones_f, scc[:, k:k + 1], None, ALU.mult, None, [Bm, Bconst], [Bm])
                modA = s2("modA", [128, 3 * D])
                modC = s2("modC", [128, 3 * D])
                abr = s2("abr", [128, 3 * D])
                ngr = s2("ngr", [128, D])
                Bab = Buf()
                P.dma(abr, ada_b[l].partition_broadcast(128), writes=[Bab], disjoint=True)
                P.dma(ngr, norm_g[l].partition_broadcast(128), writes=[Bab], disjoint=True)
                awr = Ring(nc, st2, "aw", [128, 8, 512], F32, 2)
                awbr = Ring(nc, st2, "awb", [128, 8, 512], BF16, 2)
                Bmod = Buf()
                for n in range(6):
                    aw0, Baw0 = awr.next()
                    P.dma(aw0, ada_w[l][:, n * 512:(n + 1) * 512].rearrange("(k p) c -> p k c", p=128), writes=[Baw0])
                    aw, Baw = awbr.next()
                    P.cp(aw, aw0, [Baw0], [Baw], eng="pool")
                    for wi, (rp, md) in enumerate(((rep, modA), (repc, modC))):
                        pb = wi
                        for k in range(8):
                            P.mm(ps[pb], rp[:, k, :], aw[:, k, :], k == 0, k == 7, [Bm, Baw], [Bps[pb]])
                        P.tt(md[:, n * 512:(n + 1) * 512], ps[pb], abr[:, n * 512:(n + 1) * 512], ALU.add, [Bps[pb], Bab], [Bmod], )
                G1 = s2("G1", [128, D])
                G1c = s2("G1c", [128, D])
                P.stt(G1, modA[:, D:2 * D], 1.0, ngr, ALU.add, ALU.mult, [Bmod, Bab], [Bmod])
                P.stt(G1c, modC[:, D:2 * D], 1.0, ngr, ALU.add, ALU.mult, [Bmod, Bab], [Bmod])
                P.cp(gate, modA[:, 2 * D:3 * D], [Bmod], [Bgate], eng="pool")
                P.cp(gate_c, modC[:, 2 * D:3 * D], [Bmod, Bgate], [Bgate], eng="pool")

                xr = Ring(nc, st2, "xa", [128, D], F32, 2)
                jr = Ring(nc, st2, "junk", [128, D], F32, 1)
                tr = Ring(nc, st2, "tmpa", [128, D], F32, 2)
                hr = Ring(nc, st2, "hb", [128, D], BF16, 2)
                sr = Ring(nc, st2, "ssq", [128, 4], F32, 2)
                for i in range(NTA):
                    xt, Bx = xr.next()
                    P.dma(xt, xsrc[i * 128:(i + 1) * 128, :], reads=[Bxs], writes=[Bx])
                    jk, Bj = jr.next()
                    ss, Bs = sr.next()
                    P.act(jk, xt, AF.Square, [Bx], [Bj, Bs], accum_out=ss[:, 0:1])
                    rstd_from(ss[:, 2:3], ss[:, 0:1], 1.0 / D, [Bs], [Bs], ss[:, 1:2], Bs)
                    tm, Bt = tr.next()
                    g1, sh = (G1c, modC[:, 0:D]) if i < 2 else (G1, modA[:, 0:D])
                    P.stt(tm, xt, ss[:, 2:3], g1, ALU.mult, ALU.mult, [Bx, Bs, Bmod], [Bt])
                    hb, Bh = hr.next()
                    P.tt(hb, tm, sh, ALU.add, [Bt, Bmod], [Bh], eng="pool")
                    pT = ps[2 + (i % 2)].bitcast(BF16)
                    BpT = Bps[2 + (i % 2)]
                    for k in range(8):
                        P.op("pe", lambda e: e.transpose(out=pT[:, k * 128:(k + 1) * 128], in_=hb[:, k * 128:(k + 1) * 128], identity=ident), [Bh, Bconst], [BpT])
                    P.act(hT[:, :, i * 128:(i + 1) * 128], pT.rearrange("p (k t) -> p k t", k=8), AF.Copy, [BpT], [BhT])
            P.barrier()
            if STOP == "A":
                raise _Stop

            with contextlib.ExitStack() as st2:
                def s2(name, shape, dt=F32):
                    return st2.enter_context(nc.sbuf_tensor(_u(name), list(shape), dt)).ap()
                wst = Ring(nc, st2, "wst", [128, 8, 128], F32, 2)
                wbr = Ring(nc, st2, "wb", [128, 8, 128], BF16, 2)
                cosr = Ring(nc, st2, "cosr", [128, 512], F32, 3)
                sinr = Ring(nc, st2, "sinr", [128, 512], F32, 3)
                f1 = Ring(nc, st2, "f1", [128, 512], F32, 3)
                f2 = Ring(nc, st2, "f2", [128, 512], F32, 3)
                f3 = Ring(nc, st2, "f3", [128, 512], F32, 3)
                b1r = Ring(nc, st2, "b1", [128, 512], BF16, 3)
                sqb = Ring(nc, st2, "sqb", [128, 512], BF16, 3)
                obr = Ring(nc, st2, "ob", [128, 512], BF16, 3)
                ofr = Ring(nc, st2, "of", [128, 512], F32, 3)
                gcol = colsb[:, 0:2]
                swc = s2("swc", [128, 12, 3])
                sbc = colsb[:, 19:31]
                Bcol = Buf()
                P.dma(swc, hy_sw[l], reads=[Bcols], writes=[Bcol])
                uraw = s2("uraw", [128, T])
                ucv = s2("ucv", [128, T])
                x1b = s2("x1b", [128, T])
                Buraw, Bucv, Bx1b = Buf(), Buf(), Buf()
                utk = Ring(nc, st2, "utk", [128, 4, 128], BF16, 2)
                mmi = [0]
                pending = []

                def advance():
                    for g in list(pending):
                        try:
                            next(g)
                        except StopIteration:
                            pending.remove(g)

                def run_block(gen):
                    advance()
                    try:
                        next(gen)
                        pending.append(gen)
                    except StopIteration:
                        pass

                def flush():
                    while pending:
                        advance()

                worder = list(range(8)) + list(range(16, 26)) + [12 + i for i in range(4)] + [28 + i for i in range(8)] + [48 + i for i in range(4)]
                for cc_ in range(4):
                    worder += [36 + cc_, 36 + 4 + cc_, 36 + 8 + cc_]
                wpre = {}

                def _load_w(j):
                    ws, Bws = wst.next()
                    P.dma(ws, w_in[l][:, j * 128:(j + 1) * 128].rearrange("(k p) c -> p k c", p=128), writes=[Bws])
                    wb, Bwb = wbr.next()
                    P.cp(wb, ws, [Bws], [Bwb], eng="pool")
                    return wb, Bwb

                def load_w(j):
                    idx = worder.index(j)
                    if j not in wpre:
                        wpre[j] = _load_w(j)
                    r = wpre.pop(j)
                    if idx + 1 < len(worder) and worder[idx + 1] != 36:
                        nj = worder[idx + 1]
                        wpre[nj] = _load_w(nj)
                    return r

                def proj_T(wb, Bwb, t0, t1):
                    pb = mmi[0] % 4
                    mmi[0] += 1
                    n = t1 - t0
                    for k in range(8):
                        P.mm(ps[pb][:, :n], wb[:, k, :], hT[:, k, t0:t1], k == 0, k == 7, [Bwb, BhT], [Bps[pb]])
                    return ps[pb][:, :n], Bps[pb]

                def rope_tail(src, Bsrc, srcb, Bsrcb, perm, ct, Bc, sn, Bsn, dst, Bdst, t0, t1):
                    n = t1 - t0
                    P.mm(ps[4][:, :n], perm, srcb, True, True, [Bconst, Bsrcb], [Bps[4]])
                    a1, Ba1 = f2.next()
                    a2, Ba2 = f3.next()
                    P.tt(a1[:, :n], src, ct[:, :n], ALU.mult, [Bsrc, Bc], [Ba1])
                    P.tt(a2[:, :n], ps[4][:, :n], sn[:, :n], ALU.mult, [Bps[4], Bsn], [Ba2])
                    ob, Bob = obr.next()
                    P.tt(ob[:, :n], a1[:, :n], a2[:, :n], ALU.add, [Ba1, Ba2], [Bob], eng="pool")
                    P.dma(dst[:, t0:t1], ob[:, :n], reads=[Bob], writes=[Bdst], disjoint=True)

                def load_tabs(cosT, sinT, t0, t1):
                    n = t1 - t0
                    ct, Bc = cosr.next()
                    sn, Bsn = sinr.next()
                    P.dma(ct[:, :n], cosT[:, t0:t1], writes=[Bc])
                    P.dma(sn[:, :n], sinT[:, t0:t1], writes=[Bsn])
                    return ct, Bc, sn, Bsn

                def post_da(pp, Bpp, dst, Bdst, t0, t1):
                    n = t1 - t0
                    qb, Bqb = b1r.next()
                    P.act(qb[:, :n], pp, AF.Copy, [Bpp], [Bqb])
                    tabs = load_tabs(cos_da, sin_da, t0, t1)
                    yield
                    rope_tail(pp, Bpp, qb[:, :n], Bqb, perm_da, *tabs, dst, Bdst, t0, t1)

                def post_gq(pp, Bpp, gc, dst, Bdst, t0, t1):
                    n = t1 - t0
                    sq, Bsq = sqb.next()
                    P.act(sq[:, :n], pp, AF.Square, [Bpp], [Bsq])
                    tabs = load_tabs(cos_gq, sin_gq, t0, t1)
                    yield
                    P.mm(ps[5][:, :n], ones_b, sq[:, :n], True, True, [Bconst, Bsq], [Bps[5]])
                    r1, Br1 = f2.next()
                    rs, Brs = f3.next()
                    rstd_from(rs[:, :n], ps[5][:, :n], 1.0 / 128, [Bps[5]], [Brs], r1[:, :n], Br1)
                    qn, Bqn = f1.next()
                    P.stt(qn[:, :n], pp, gc, rs[:, :n], ALU.mult, ALU.mult, [Bpp, Bcols, Brs], [Bqn])
                    qb, Bqb = b1r.next()
                    P.cp(qb[:, :n], qn[:, :n], [Bqn], [Bqb], eng="pool")
                    yield
                    rope_tail(qn[:, :n], Bqn, qb[:, :n], Bqb, perm_gq, *tabs, dst, Bdst, t0, t1)

                for j in range(8):
                    wb, Bwb = load_w(j)
                    dst, Bdst = (QTda, BQTda) if j < 4 else (KTda, BKTda)
                    for (t0, t1) in BLK:
                        pp, Bpp = proj_T(wb, Bwb, t0, t1)
                        run_block(post_da(pp, Bpp, dst[j % 4], Bdst, t0, t1))
                for j in list(range(16, 26)):
                    wb, Bwb = load_w(j)
                    isq = j < 24
                    dst, Bdst = (QTgq[j - 16], BQTgq) if isq else (KTgq[j - 24], BKTgq)
                    gc = gcol[:, 0:1] if isq else gcol[:, 1:2]
                    for (t0, t1) in BLK:
                        pp, Bpp = proj_T(wb, Bwb, t0, t1)
                        run_block(post_gq(pp, Bpp, gc, dst, Bdst, t0, t1))
                for j, grow in [(12 + i, i * 128) for i in range(4)] + [(28 + i, 512 + i * 128) for i in range(8)] + [(48 + i, 1536 + i * 128) for i in range(4)]:
                    wb, Bwb = load_w(j)
                    for (t0, t1) in BLK:
                        n = t1 - t0
                        pp, Bpp = proj_T(wb, Bwb, t0, t1)
                        advance()
                        of, Bof = ofr.next()
                        P.act(of[:, :n], pp, AF.Silu, [Bpp], [Bof])
                        P.dma(GT[grow:grow + 128, t0:t1], of[:, :n], reads=[Bof], writes=[BGT], disjoint=True)
                flush()
                wvb = s2("wvb", [128, 8, 512], BF16)
                Bwvb = Buf()
                for (c0, nv, dst, Bdst) in ((1024, 512, Vda, BVda), (3328, 256, Vgq, BVgq)):
                    for c4 in range(nv // 128):
                        ws, Bws = wst.next()
                        P.dma(ws, w_in[l][:, c0 + c4 * 128:c0 + (c4 + 1) * 128].rearrange("(k p) c -> p k c", p=128), writes=[Bws])
                        P.cp(wvb[:, :, c4 * 128:(c4 + 1) * 128], ws, [Bws], [Bwvb], eng="pool")
                    for i in range(NT):
                        pb = mmi[0] % 4
                        mmi[0] += 1
                        for k in range(8):
                            P.mm(ps[pb][:, :nv], hT[:, k, i * 128:(i + 1) * 128], wvb[:, k, :nv], k == 0, k == 7, [Bwvb, BhT], [Bps[pb]])
                        ob, Bob = obr.next()
                        P.act(ob[:, :nv], ps[pb][:, :nv], AF.Copy, [Bps[pb]], [Bob])
                        P.dma(dst[i * 128:(i + 1) * 128, :], ob[:, :nv], reads=[Bob], writes=[Bdst], disjoint=True)

                def conv_chunk(jj, outbuf, Boutb):
                    wb, Bwb = load_w(36 + jj)
                    for (t0, t1) in BLK:
                        pp, Bpp = proj_T(wb, Bwb, t0, t1)
                        advance()
                        P.act(uraw[:, t0:t1], pp, AF.Copy, [Bpp], [Buraw], )
                        P.act(outbuf[:, t0:t1], pp, AF.Identity, [Bpp, Bcol, Bcols], [Boutb], scale=swc[:, jj, 1:2], bias=sbc[:, jj:jj + 1])
                    for (s0, s1) in ((0, LC), (LC, T)):
                        P.stt(outbuf[:, s0 + 1:s1], uraw[:, s0:s1 - 1], swc[:, jj, 0:1], outbuf[:, s0 + 1:s1], ALU.mult, ALU.add, [Buraw, Bcol, Boutb], [Boutb])
                        P.stt(outbuf[:, s0:s1 - 1], uraw[:, s0 + 1:s1], swc[:, jj, 2:3], outbuf[:, s0:s1 - 1], ALU.mult, ALU.add, [Buraw, Bcol, Boutb], [Boutb])

                def tr_gen(cc):
                    yield
                    for i0 in range(0, NT, 4):
                        ni = min(4, NT - i0)
                        pb = 6 + ((i0 // 4) % 2)
                        for ii in range(ni):
                            i = i0 + ii
                            P.op("pe", lambda e: e.transpose(out=ps[pb][:, ii * 128:(ii + 1) * 128], in_=x1b[:, i * 128:(i + 1) * 128], identity=identf), [Bx1b, Bconst], [Bps[pb]])
                        ut, But_ = utk.next()
                        P.act(ut[:, :ni, :], ps[pb][:, :ni * 128].rearrange("p (a c) -> p a c", a=ni), AF.Copy, [Bps[pb]], [But_])
                        P.dma(Utok[i0 * 128:(i0 + ni) * 128, cc * 128:(cc + 1) * 128].rearrange("(a p) c -> p a c", p=128), ut[:, :ni, :], reads=[But_], writes=[BUtok], disjoint=True)

                for cc in range(4):
                    conv_chunk(cc, ucv, Bucv)
                    flush()
                    P.dma(X0T[cc * 128:(cc + 1) * 128, :], ucv, reads=[Bucv], writes=[BX0T], disjoint=True)
                    conv_chunk(4 + cc, x1b, Bx1b)
                    conv_chunk(8 + cc, ucv, Bucv)
                    P.tt(x1b, x1b, ucv, ALU.mult, [Bx1b, Bucv], [Bx1b], eng="pool")
                    P.dma(UT[cc * 128:(cc + 1) * 128, :], x1b, reads=[Bx1b], writes=[BUT], disjoint=True)
                    g = tr_gen(cc)
                    next(g)
                    pending.append(g)
                flush()
            P.barrier()
            if STOP == "B":
                raise _Stop

        seqs = [(L, LC, T)] + ([(LC, 0, LC)] if not last else [])
        for (Ls, s0, s1) in seqs:
            ntc = Ls // 128
            NF = 2 * Ls
            with contextlib.ExitStack() as st2:
                def s2(name, shape, dt=F32):
                    return st2.enter_context(nc.sbuf_tensor(_u(name), list(shape), dt)).ap()
                zs = s2("zs", [33, Ls])
                w1s = s2("w1s", [33, 64])
                w2s = s2("w2s", [64, 64])
                w3s = s2("w3s", [64, 64])
                w4s = s2("w4s", [64, 1024])
                bs4 = s2("bs4", [64, 4])
                bs = bs4[:, 0:3]
                fq = bs4[:, 3:4]
                fb = s2("fb", [64, 3])
                Bfw = Buf()
                P.dma(zs, zT[Ls], writes=[Bfw], disjoint=True)
                P.dma(w1s, hy_w1[l], writes=[Bfw], disjoint=True)
                P.dma(w2s, hy_w2[l], writes=[Bfw], disjoint=True)
                P.dma(w3s, hy_w3[l], writes=[Bfw], disjoint=True)
                P.dma(w4s, hy_w4[l], writes=[Bfw], disjoint=True)
                P.dma(bs4, hy_b[l], writes=[Bfw], disjoint=True)
                Bfb = Buf()
                P.ts(fb, bs, fq[:, 0:1], None, ALU.mult, None, [Bfw], [Bfb])
                hbuf = [s2("hf%d" % i, [64, Ls]) for i in range(3)]
                Bhb = [Buf() for _ in range(3)]
                ar = Ring(nc, st2, "far", [64, 512], F32, 2)
                mr = Ring(nc, st2, "fmr", [64, 512], F32, 4)
                nb = min(512, Ls)
                for li in range(3):
                    src = zs if li == 0 else hbuf[li - 1]
                    Bsrc = Bfw if li == 0 else Bhb[li - 1]
                    wl = (w1s, w2s, w3s)[li]
                    for bi, t0 in enumerate(range(0, Ls, nb)):
                        pb = bi % 2
                        P.mm(ps[pb][:64, :nb], wl, src[:, t0:t0 + nb], True, True, [Bfw, Bsrc], [Bps[pb]])
                        a, Ba = ar.next()
                        P.ts(a[:, :nb], ps[pb][:64, :nb], fq[:, 0:1], fb[:, li:li + 1], ALU.mult, ALU.add, [Bps[pb], Bfw, Bfb], [Ba])
                        for (cmp_, thr, add_) in ((ALU.is_gt, math.pi, -2 * math.pi), (ALU.is_lt, -math.pi, 2 * math.pi)):
                            m, Bm_ = mr.next()
                            P.ts(m[:, :nb], a[:, :nb], thr, add_, cmp_, ALU.mult, [Ba], [Bm_])
                            m2, Bm2 = mr.next()
                            P.tt(m2[:, :nb], a[:, :nb], m[:, :nb], ALU.add, [Ba, Bm_], [Bm2])
                            a, Ba = m2, Bm2
                        P.act(hbuf[li][:, t0:t0 + nb], a[:, :nb], AF.Sin, [Ba], [Bhb[li]])
                dr = Ring(nc, st2, "fdr", [128, 512], F32, 2)
                fr = Ring(nc, st2, "ffr", [128, 1024], F32, 2)
                sqr = Ring(nc, st2, "fsq", [128, 1024], F32, 2)
                fbr = Ring(nc, st2, "ffb", [128, 1024], BF16, 2)
                for tc in range(ntc):
                    dc, Bdc = dr.next()
                    P.dma(dc, decay[Ls][tc * 128:(tc + 1) * 128, :], writes=[Bdc])
                    f, Bf = fr.next()
                    for hf in range(2):
                        pb = hf
                        P.mm(ps[pb], hbuf[2][:, tc * 128:(tc + 1) * 128], w4s[:, hf * 512:(hf + 1) * 512], True, True, [Bhb[2], Bfw], [Bps[pb]])
                        P.tt(f[:, hf * 512:(hf + 1) * 512], ps[pb], dc, ALU.mult, [Bps[pb], Bdc], [Bf])
                    if tc == 0:
                        P.op("dve", lambda e: e.memset(f[0:1, 512:1024], 0.0), [Bf], [Bf])
                    sq, Bsq = sqr.next()
                    P.act(sq, f, AF.Square, [Bf], [Bsq])
                    for hf in range(2):
                        P.mm(ps[2], ones_f, sq[:, hf * 512:(hf + 1) * 512], tc == 0 and hf == 0, tc == ntc - 1 and hf == 1, [Bconst, Bsq], [Bps[2]])
                    fbt, Bfbt = fbr.next()
                    P.tt(fbt[:, 0:512], f[:, 0:512], f[:, 512:1024], ALU.add, [Bf], [Bfbt], eng="pool")
                    P.tt(fbt[:, 512:1024], f[:, 0:512], f[:, 512:1024], ALU.subtract, [Bf, Bfbt], [Bfbt], eng="pool")
                    P.dma(Ftok[tc * 128:(tc + 1) * 128, :], fbt, reads=[Bfbt], writes=[BFtok], disjoint=True)
                rn1 = s2("rn1", [128, 512])
                rn2 = s2("rn2", [128, 512])
                Brn1, Brn2 = Buf(), Buf()
                rstd_from(rn2, ps[2], 1.0, [Bps[2]], [Brn2], rn1, Brn1)
                P.dma(RN, rn2, reads=[Brn2], writes=[BRN])
            P.barrier()
            if STOP == "F":
                raise _Stop

            with contextlib.ExitStack() as st2:
                def s2(name, shape, dt=F32):
                    return st2.enter_context(nc.sbuf_tensor(_u(name), list(shape), dt)).ap()
                ud = s2("ud", [128, ntc, 512], BF16)
                fd = s2("fd", [128, ntc, 1024], BF16)
                rn = s2("rn", [128, 512])
                Bd = Buf()
                P.dma(ud, Utok[s0:s1, :].rearrange("(a p) c -> p a c", p=128), reads=[BUtok], writes=[Bd], disjoint=True)
                P.dma(fd, Ftok[0:Ls, :].rearrange("(a p) c -> p a c", p=128), reads=[BFtok], writes=[Bd], disjoint=True)
                P.dma(rn, RN, reads=[BRN], writes=[Bd], disjoint=True)
                cr = Ring(nc, st2, "dcr", [128, Ls], BF16, 2)
                srr = Ring(nc, st2, "dsr", [128, Ls], BF16, 2)
                e = [Ring(nc, st2, "de%d" % i, [128, 512], F32, 2) for i in range(10)]
                yb = [Ring(nc, st2, "dy%d" % i, [128, 512], BF16, 2) for i in range(2)]
                dpre = {}

                def dload(kc_):
                    cm_, Bcm_ = cr.next()
                    sm_, Bsm_ = srr.next()
                    P.dma(cm_, Cf[Ls][kc_], writes=[Bcm_])
                    P.dma(sm_, Sf[Ls][kc_], writes=[Bsm_])
                    dpre[kc_] = (cm_, Bcm_, sm_, Bsm_)

                dload(0)
                for kc in range(ntc):
                    if kc + 1 < ntc:
                        dload(kc + 1)
                    cm, Bcm, sm, Bsm = dpre.pop(kc)
                    base = (4 * kc) % 8
                    bk = [base + i_ for i_ in range(4)]
                    if kc == 0:
                        for tc in range(ntc):
                            P.mm(ps[7][0:1, :], sm[:, tc * 128:tc * 128 + 1], fd[:, tc, 0:512], tc == 0, tc == ntc - 1, [Bsm, Bd], [Bps[7]])
                        nyq, Bnyq = e[2].next()
                        P.act(nyq[0:1, :], ps[7][0:1, :], AF.Copy, [Bps[7]], [Bnyq])
                    for si2, (mat, Bmat, di) in enumerate(((cm, Bcm, 0), (cm, Bcm, 1), (sm, Bsm, 0), (sm, Bsm, 2))):
                        pb = bk[si2]
                        for tc in range(ntc):
                            data = ud[:, tc, :] if di == 0 else fd[:, tc, (di - 1) * 512:di * 512]
                            P.mm(ps[pb], mat[:, tc * 128:(tc + 1) * 128], data, tc == 0, tc == ntc - 1, [Bmat, Bd], [Bps[pb]])
                    ure, Bure = e[0].next()
                    uim, Buim = e[1].next()
                    P.act(ure, ps[bk[0]], AF.Copy, [Bps[bk[0]]], [Bure])
                    P.act(uim, ps[bk[2]], AF.Copy, [Bps[bk[2]]], [Buim])
                    hre, Bhre = e[4].next()
                    him, Bhim = e[5].next()
                    P.tt(hre, ps[bk[1]], rn, ALU.mult, [Bps[bk[1]], Bd], [Bhre])
                    P.tt(him, ps[bk[3]], rn, ALU.mult, [Bps[bk[3]], Bd], [Bhim])
                    if kc == 0:
                        P.tt(him[0:1, :], nyq[0:1, :], rn[0:1, :], ALU.mult, [Bnyq, Bd, Bhim], [Bhim])
                    t1, Bt1 = e[6].next()
                    t2, Bt2 = e[7].next()
                    t3, Bt3 = e[8].next()
                    t4, Bt4 = e[9].next()
                    P.tt(t1, ure, hre, ALU.mult, [Bure, Bhre], [Bt1], eng="pool")
                    P.tt(t2, uim, him, ALU.mult, [Buim, Bhim], [Bt2], eng="pool")
                    P.tt(t3, ure, him, ALU.mult, [Bure, Bhim], [Bt3], eng="pool")
                    P.tt(t4, uim, hre, ALU.mult, [Buim, Bhre], [Bt4], eng="pool")
                    yre, Byre = yb[0].next()
                    yim, Byim = yb[1].next()
                    P.tt(yre, t1, t2, ALU.subtract, [Bt1, Bt2], [Byre])
                    P.tt(yim, t3, t4, ALU.add, [Bt3, Bt4], [Byim])
                    if kc == 0:
                        P.ts(yre[0:1, :], t1[0:1, :], 0.5, None, ALU.mult, None, [Bt1, Byre], [Byre])
                        P.ts(yim[0:1, :], t2[0:1, :], 0.5, None, ALU.mult, None, [Bt2, Byim], [Byim])
                    P.dma(Yf[0, kc * 128:(kc + 1) * 128, :], yre, reads=[Byre], writes=[BYf], disjoint=True)
                    P.dma(Yf[1, kc * 128:(kc + 1) * 128, :], yim, reads=[Byim], writes=[BYf], disjoint=True)
            P.barrier()
            if STOP == "D":
                raise _Stop

            with contextlib.ExitStack() as st2:
                def s2(name, shape, dt=F32):
                    return st2.enter_context(nc.sbuf_tensor(_u(name), list(shape), dt)).ap()
                yy = s2("yy", [128, 2, ntc, 512], BF16)
                By = Buf()
                for ri in range(2):
                    P.dma(yy[:, ri], Yf[ri, 0:Ls, :].rearrange("(a p) c -> p a c", p=128), reads=[BYf], writes=[By], disjoint=True)
                skc = colsb[:, 11:15]
                ogc = colsb[:, 15:19]
                nh = 2 if ntc >= 16 else 1
                hk = ntc // nh
                nbk = min(512, Ls)
                cir = Ring(nc, st2, "icr", [128, hk, nbk], BF16, 2)
                sir = Ring(nc, st2, "isr", [128, hk, nbk], BF16, 2)
                lu = Ring(nc, st2, "ilu", [128, 512], F32, 2)
                lx = Ring(nc, st2, "ilx", [128, 512], F32, 2)
                lg = Ring(nc, st2, "ilg", [128, 512], F32, 4)
                tq = Ring(nc, st2, "itq", [128, 512], F32, 2)
                ocr = Ring(nc, st2, "ioc", [128, 512], F32, 8)
                ob2 = Ring(nc, st2, "iob", [128, 512], BF16, 3)
                r1r = Ring(nc, st2, "ir1", [128, 512], F32, 2)
                sqi = Ring(nc, st2, "isq", [128, 512], BF16, 8)
                ipend = []

                def iadvance():
                    for g in list(ipend):
                        try:
                            next(g)
                        except StopIteration:
                            ipend.remove(g)

                def epi(tb, bset):
                    g0 = s0 + tb
                    ocs = []
                    sqs = []
                    for cc in range(4):
                        u_, Bu_ = lu.next()
                        x_, Bx_ = lx.next()
                        P.dma(u_[:, :nbk], UT[cc * 128:(cc + 1) * 128, g0:g0 + nbk], reads=[BUT], writes=[Bu_])
                        P.dma(x_[:, :nbk], X0T[cc * 128:(cc + 1) * 128, g0:g0 + nbk], reads=[BX0T], writes=[Bx_])
                        t_, Bt_ = tq.next()
                        P.ts(t_[:, :nbk], u_[:, :nbk], skc[:, cc:cc + 1], None, ALU.mult, None, [Bu_, Bcols], [Bt_], eng="pool")
                        oc, Boc = ocr.next()
                        P.stt(oc[:, :nbk], ps[bset[cc]][:, :nbk], 2.0 / NF, t_[:, :nbk], ALU.mult, ALU.add, [Bps[bset[cc]], Bt_], [Boc])
                        P.tt(oc[:, :nbk], oc[:, :nbk], x_[:, :nbk], ALU.mult, [Boc, Bx_], [Boc], eng="pool")
                        sq, Bsq = sqi.next()
                        P.act(sq[:, :nbk], oc[:, :nbk], AF.Square, [Boc], [Bsq])
                        ocs.append((oc, Boc))
                        sqs.append((sq, Bsq))
                    yield
                    pq = bset[0]
                    for cc in range(4):
                        sq, Bsq = sqs[cc]
                        P.mm(ps[pq][:, :nbk], ones_b, sq[:, :nbk], cc == 0, cc == 3, [Bconst, Bsq], [Bps[pq]])
                    r1, Br1 = r1r.next()
                    rs, Brs = r1r.next()
                    rstd_from(rs[:, :nbk], ps[pq][:, :nbk], 1.0 / 512, [Bps[pq]], [Brs], r1[:, :nbk], Br1)
                    for cc in range(4):
                        oc, Boc = ocs[cc]
                        g_, Bg_ = lg.next()
                        P.dma(g_[:, :nbk], GT[1536 + cc * 128:1536 + (cc + 1) * 128, g0:g0 + nbk], reads=[BGT], writes=[Bg_])
                        P.stt(oc[:, :nbk], oc[:, :nbk], ogc[:, cc:cc + 1], rs[:, :nbk], ALU.mult, ALU.mult, [Boc, Bcols, Brs], [Boc])
                        ob, Bob = ob2.next()
                        P.tt(ob[:, :nbk], oc[:, :nbk], g_[:, :nbk], ALU.mult, [Boc, Bg_], [Bob], eng="pool")
                        P.dma(YMT[1536 + cc * 128:1536 + (cc + 1) * 128, g0:g0 + nbk], ob[:, :nbk], reads=[Bob], writes=[BYMT], disjoint=True)

                ipre = {}

                def iload(tb_, h_):
                    cm_, Bcm_ = cir.next()
                    sm_, Bsm_ = sir.next()
                    P.dma(cm_, Ci[Ls][:, h_ * hk:(h_ + 1) * hk, tb_:tb_ + nbk], writes=[Bcm_])
                    P.dma(sm_, Si[Ls][:, h_ * hk:(h_ + 1) * hk, tb_:tb_ + nbk], writes=[Bsm_])
                    ipre[(tb_, h_)] = (cm_, Bcm_, sm_, Bsm_)

                for tbi, tb in enumerate(range(0, Ls, nbk)):
                    bset = (0, 1, 2, 3) if tbi % 2 == 0 else (4, 5, 6, 7)
                    for h in range(nh):
                        if (tb, h) not in ipre:
                            iload(tb, h)
                        nxt = (tb, h + 1) if h + 1 < nh else (tb + nbk, 0)
                        if nxt[0] < Ls:
                            iload(*nxt)
                        cm, Bcm, sm, Bsm = ipre.pop((tb, h))
                        for cc in range(4):
                            pc = bset[cc]
                            for kk in range(hk):
                                kc = h * hk + kk
                                P.mm(ps[pc][:, :nbk], yy[:, 0, kc, cc * 128:(cc + 1) * 128], cm[:, kk, :], kc == 0, False, [By, Bcm], [Bps[pc]])
                                P.mm(ps[pc][:, :nbk], yy[:, 1, kc, cc * 128:(cc + 1) * 128], sm[:, kk, :], False, kc == ntc - 1, [By, Bsm], [Bps[pc]])
                        if h == 0:
                            iadvance()
                    g = epi(tb, bset)
                    next(g)
                    ipend.append(g)
                while ipend:
                    iadvance()
            P.barrier()
            if STOP == "I":
                raise _Stop

        qsets = [(BLK[1:], 0, T)] + ([([BLK[0]], 0, LC)] if not last else [])
        for which in ("da", "gq"):
            with contextlib.ExitStack() as st2:
                def s2(name, shape, dt=F32):
                    return st2.enter_context(nc.sbuf_tensor(_u(name), list(shape), dt)).ap()
                nkv = 4 if which == "da" else 2
                ncomp = 2 if which == "da" else 1
                nheads = 4 if which == "da" else 8
                KTs, BKTs = (KTda, BKTda) if which == "da" else (KTgq, BKTgq)
                Vs, BVs = (Vda, BVda) if which == "da" else (Vgq, BVgq)
                QTs, BQTs = (QTda, BQTda) if which == "da" else (QTgq, BQTgq)
                scale = 0.125 if which == "da" else 128 ** -0.5
                kt = s2("kt", [128, nkv, T], BF16)
                vt = s2("vt", [128, NT, nkv * 128], BF16)
                Bkv = Buf()
                for h in range(nkv):
                    P.dma(kt[:, h, :], KTs[h], reads=[BKTs], writes=[Bkv], disjoint=True)
                P.dma(vt, Vs.rearrange("(a p) c -> p a c", p=128), reads=[BVs], writes=[Bkv], disjoint=True)
                gcol2 = s2("gcol2", [128, 8])
                Bg2 = Buf()
                if which == "da":
                    P.ts(gcol2[:, 0:1], colsb[:, 2:3], 1.0 - lam_init, None, ALU.mult, None, [Bcols], [Bg2])
                else:
                    P.cp(gcol2, colsb[:, 3:11], [Bcols], [Bg2])
                qr = Ring(nc, st2, "aq", [128, 512], BF16, 3)
                qr1 = Ring(nc, st2, "aq1", [128, 512], BF16, 3)
                if which == "da":
                    for rr in (qr, qr1):
                        for tq_, bq_ in zip(rr.t, rr.b):
                            P.op("dve", lambda e: e.memset(tq_, 0.0), writes=[bq_])
                pr = Ring(nc, st2, "ap", [128, 512], BF16, 5)
                evr = Ring(nc, st2, "aev", [128, 512], F32, 4)
                rcr = Ring(nc, st2, "arc", [128, 512], F32, 4)
                ohr = Ring(nc, st2, "aoh", [128, 512], F32, 10)
                sqr2 = Ring(nc, st2, "asq", [128, 512], BF16, 2)
                r1r = Ring(nc, st2, "ar1", [128, 512], F32, 2)
                glr = Ring(nc, st2, "agl", [128, 512], F32, 3)
                obr2 = Ring(nc, st2, "aob", [128, 512], BF16, 3)
                steps = []
                ja = 0
                for (qblocks, k0, k1) in qsets:
                    for (t0, t1) in qblocks:
                        blk = {"heads_o": [], "q": {}, "comps": {}}
                        for h in range(nheads):
                            for c in range(ncomp):
                                job = dict(t0=t0, t1=t1, n=t1 - t0, h=h, c=c, k0=k0, nk=(k1 - k0) // 128, blk=blk,
                                           kvh=(h if which == "da" else h // 4), pO=2 + (ja % 2) * 2)
                                ja += 1
                                for ki in range(job["nk"]):
                                    steps.append((job, ki))

                def emit_qk(si_, job, ki):
                    n, h, c, t0, t1 = job["n"], job["h"], job["c"], job["t0"], job["t1"]
                    blk = job["blk"]
                    if h not in blk["q"]:
                        q_, Bq_ = qr.next()
                        if which == "da":
                            q1_, Bq1_ = qr1.next()
                            P.dma(q_[0:64, :n], QTs[h][0:64, t0:t1], reads=[BQTs], writes=[Bq_])
                            P.dma(q1_[64:128, :n], QTs[h][64:128, t0:t1], reads=[BQTs], writes=[Bq1_])
                            blk["q"][h] = ((q_, Bq_), (q1_, Bq1_))
                        else:
                            P.dma(q_[:, :n], QTs[h][:, t0:t1], reads=[BQTs], writes=[Bq_])
                            blk["q"][h] = ((q_, Bq_),)
                    q_, Bq_ = blk["q"][h][c]
                    kk0 = job["k0"] + ki * 128
                    pS = SB[si_ % 3]
                    P.mm(ps[pS][:, :n], kt[:, job["kvh"], kk0:kk0 + 128], q_[:, :n], True, True, [Bkv, Bq_], [Bps[pS]])

                def emit_rest(si_, job, ki):
                    n, h, c, t0, t1, nk, kvh = job["n"], job["h"], job["c"], job["t0"], job["t1"], job["nk"], job["kvh"]
                    blk = job["blk"]
                    pO = job["pO"]
                    pD = pO + 1
                    pS = SB[si_ % 3]
                    kk0 = job["k0"] + ki * 128
                    p_, Bp_ = pr.next()
                    P.act(p_[:, :n], ps[pS][:, :n], AF.Exp, [Bps[pS]], [Bp_], scale=scale)
                    P.mm(ps[pO][:, :n], vt[:, kk0 // 128, kvh * 128:(kvh + 1) * 128], p_[:, :n], ki == 0, ki == nk - 1, [Bkv, Bp_], [Bps[pO]])
                    P.mm(ps[pD][:, :n], ones_b, p_[:, :n], ki == 0, ki == nk - 1, [Bconst, Bp_], [Bps[pD]])
                    if ki != nk - 1:
                        return
                    rc, Brc = rcr.next()
                    P.op("dve", lambda e: e.reciprocal(out=rc[:, :n], in_=ps[pD][:, :n]), [Bps[pD]], [Brc])
                    ev, Bev = evr.next()
                    P.tt(ev[:, :n], ps[pO][:, :n], rc[:, :n], ALU.mult, [Bps[pO], Brc], [Bev])
                    blk["comps"][(h, c)] = (ev, Bev)
                    if c != ncomp - 1:
                        return
                    oh, Boh = ohr.next()
                    if which == "da":
                        e0, Be0 = blk["comps"][(h, 0)]
                        e1, Be1 = blk["comps"][(h, 1)]
                        P.stt(oh[:, :n], e1[:, :n], lam_c[:, 0:1], e0[:, :n], ALU.mult, ALU.add, [Be1, Be0, Blam], [Boh])
                    else:
                        e0, Be0 = blk["comps"][(h, 0)]
                        P.cp(oh[:, :n], e0[:, :n], [Be0], [Boh], eng="pool")
                    sq, Bsq = sqr2.next()
                    P.act(sq[:, :n], oh[:, :n], AF.Square, [Boh], [Bsq])
                    if which == "da":
                        P.mm(ps[6][:, :n], ones_b, sq[:, :n], True, True, [Bconst, Bsq], [Bps[6]])
                        heads_fin = [(h, oh, Boh)]
                        dn = 128
                    else:
                        P.mm(ps[6][:, :n], ones_b, sq[:, :n], h == 0, h == 7, [Bconst, Bsq], [Bps[6]])
                        blk["heads_o"].append((h, oh, Boh))
                        heads_fin = blk["heads_o"] if h == 7 else []
                        dn = 1024
                    if heads_fin:
                        r1, Br1 = r1r.next()
                        rs, Brs = r1r.next()
                        rstd_from(rs[:, :n], ps[6][:, :n], 1.0 / dn, [Bps[6]], [Brs], r1[:, :n], Br1)
                        for (hh, o_, Bo_) in heads_fin:
                            grow = hh * 128 if which == "da" else 512 + hh * 128
                            gcc = gcol2[:, 0:1] if which == "da" else gcol2[:, hh:hh + 1]
                            g_, Bg_ = glr.next()
                            P.dma(g_[:, :n], GT[grow:grow + 128, t0:t1], reads=[BGT], writes=[Bg_])
                            P.stt(o_[:, :n], o_[:, :n], gcc, rs[:, :n], ALU.mult, ALU.mult, [Bo_, Bg2, Brs], [Bo_])
                            ob, Bob = obr2.next()
                            P.tt(ob[:, :n], o_[:, :n], g_[:, :n], ALU.mult, [Bo_, Bg_], [Bob], eng="pool")
                            P.dma(YMT[grow:grow + 128, t0:t1], ob[:, :n], reads=[Bob], writes=[BYMT], disjoint=True)

                SB = (0, 1, 7)
                emit_qk(0, *steps[0])
                if len(steps) > 1:
                    emit_qk(1, *steps[1])
                for si_ in range(len(steps)):
                    if si_ + 2 < len(steps):
                        emit_qk(si_ + 2, *steps[si_ + 2])
                    emit_rest(si_, *steps[si_])
            P.barrier()
            if STOP == which:
                raise _Stop

        with contextlib.ExitStack() as st2:
            def s2(name, shape, dt=F32):
                return st2.enter_context(nc.sbuf_tensor(_u(name), list(shape), dt)).ap()
            wos = s2("wos", [128, 16, 512], F32)
            wob = s2("wob", [128, 16, D], BF16)
            Bwos, Bwob = Buf(), Buf()
            for hf in range(2):
                P.dma(wos, w_out[l][:, hf * 512:(hf + 1) * 512].rearrange("(k p) c -> p k c", p=128), reads=[Bwob], writes=[Bwos])
                P.cp(wob[:, :, hf * 512:(hf + 1) * 512], wos, [Bwos], [Bwob], eng="pool")
            fgr = s2("fgr", [128, D])
            Bfg = Buf()
            P.dma(fgr, final_g.partition_broadcast(128), writes=[Bfg])
            ymr = Ring(nc, st2, "oym", [128, 16, 128], BF16, 3)
            xr = Ring(nc, st2, "ox", [128, D], F32, 3)
            yr = Ring(nc, st2, "oy", [128, D], F32, 3)
            jr = Ring(nc, st2, "oj", [128, D], F32, 1)
            sr = Ring(nc, st2, "oss", [128, 4], F32, 2)
            tiles = range(2, NT) if last else range(NT)
            opre = {}

            def oload(i_):
                ym_, Bym_ = ymr.next()
                P.dma(ym_, YMT[:, i_ * 128:(i_ + 1) * 128].rearrange("(k p) t -> p k t", p=128), reads=[BYMT], writes=[Bym_])
                xt_, Bx_ = xr.next()
                P.dma(xt_, xsrc[i_ * 128:(i_ + 1) * 128, :], reads=[Bxs], writes=[Bx_])
                opre[i_] = (ym_, Bym_, xt_, Bx_)

            tiles = list(tiles)
            oload(tiles[0])
            for ti_, i in enumerate(tiles):
                if ti_ + 1 < len(tiles):
                    oload(tiles[ti_ + 1])
                ym, Bym, xt, Bx = opre.pop(i)
                y_, By_ = yr.next()
                gt = gate_c if i < 2 else gate
                for hf in range(2):
                    pb = (i % 2) * 2 + hf
                    for k in range(16):
                        P.mm(ps[pb], ym[:, k, :], wob[:, k, hf * 512:(hf + 1) * 512], k == 0, k == 15, [Bym, Bwob], [Bps[pb]])
                    P.tt(y_[:, hf * 512:(hf + 1) * 512], ps[pb], gt[:, hf * 512:(hf + 1) * 512], ALU.mult, [Bps[pb], Bgate], [By_])
                P.tt(y_, y_, xt, ALU.add, [By_, Bx], [By_], eng="pool")
                if not last:
                    P.dma(xs[i * 128:(i + 1) * 128, :], y_, reads=[By_], writes=[Bxs], disjoint=True)
                else:
                    jk, Bj = jr.next()
                    ss, Bs = sr.next()
                    P.act(jk, y_, AF.Square, [By_], [Bj, Bs], accum_out=ss[:, 0:1])
                    rstd_from(ss[:, 2:3], ss[:, 0:1], 1.0 / D, [Bs], [Bs], ss[:, 1:2], Bs)
                    P.stt(y_, y_, ss[:, 2:3], fgr, ALU.mult, ALU.mult, [By_, Bs, Bfg], [By_])
                    P.dma(out[(i - 2) * 128:(i - 1) * 128, :], y_, reads=[By_], writes=[Bout], disjoint=True)
        P.barrier()
      except _Stop:
        break
    P.barrier()
    return nc


def _colmajor(v):
    return np.ascontiguousarray(v.reshape(-1, 128).T)


def _consts():
    bf = ml_dtypes.bfloat16
    c = {}
    c["ident"] = np.eye(128, dtype=np.float32).astype(bf)
    c["identf"] = np.eye(128, dtype=np.float32)
    p = np.arange(128)
    pd = np.zeros((128, 128), np.float32)
    part = np.where((p % 32) < 16, p + 16, p - 16)
    pd[part, p] = 1.0
    c["perm_da"] = pd.astype(bf)
    pg = np.zeros((128, 128), np.float32)
    partg = np.where((p % 64) < 32, p + 32, p - 32)
    pg[partg, p] = 1.0
    c["perm_gq"] = pg.astype(bf)
    tl = np.arange(L)
    row = (tl // 64).astype(np.float64)
    col = (tl % 64).astype(np.float64)

    def tables(head_dim, npart_rep):
        axis = head_dim // 2
        inv = 10000.0 ** (-np.arange(0, axis, 2, dtype=np.float64) / axis)
        nf = axis // 2
        cosT = np.ones((head_dim, T), np.float64)
        sinT = np.zeros((head_dim, T), np.float64)
        for d in range(head_dim):
            half = d // axis
            j = d % nf
            isx2 = (d % axis) // nf
            pos = row if half == 0 else col
            ang = pos * inv[j]
            cosT[d, LC:] = np.cos(ang)
            sinT[d, LC:] = np.sin(ang) * (1.0 if isx2 else -1.0)
        cosT = np.tile(cosT, (npart_rep, 1))
        sinT = np.tile(sinT, (npart_rep, 1))
        return cosT.astype(np.float32), sinT.astype(np.float32)

    c["cos_da"], c["sin_da"] = tables(64, 2)
    c["cos_gq"], c["sin_gq"] = tables(128, 1)
    for (Ls, sfx) in ((L, "l"), (LC, "c")):
        t = np.linspace(0.0, 1.0, Ls, dtype=np.float32).astype(np.float64)
        w = 2.0 * math.pi * np.arange(Ls, dtype=np.float64) / Ls
        f = np.linspace(1e-4, 15.0, 16, dtype=np.float32).astype(np.float64)
        z = np.concatenate([t[:, None], np.cos(f[None] * w[:, None]), -np.sin(f[None] * w[:, None])], axis=1)
        c["zT_" + sfx] = np.ascontiguousarray(z.T).astype(np.float32)
        max_decay = math.log(1e-2) / 0.3
        min_decay = math.log(1e-2) / 1.5
        deltas = np.linspace(min_decay, max_decay, 512, dtype=np.float32).astype(np.float64)
        c["decay_" + sfx] = np.exp(-t[:, None] * np.abs(deltas)[None]).astype(np.float32)
        N = 2 * Ls
        tt = np.arange(Ls, dtype=np.int64)
        prod = (tt[:, None] * tt[None, :]) % N
        ang = 2.0 * math.pi * prod.astype(np.float64) / N
        C = np.cos(ang).astype(np.float32)
        S = (-np.sin(ang)).astype(np.float32)
        alt = np.where(tt % 2 == 0, 1.0, -1.0).astype(np.float32)
        Sfm = S.copy()
        Sfm[:, 0] = alt
        Sim = Sfm.T.copy()
        ntc = Ls // 128
        def fwd_arr(M):
            a = M.reshape(ntc, 128, ntc, 128)
            return np.ascontiguousarray(a.transpose(2, 1, 0, 3).reshape(ntc, 128, Ls)).astype(bf)
        def inv_arr(M):
            a = M.reshape(ntc, 128, Ls)
            return np.ascontiguousarray(a.transpose(1, 0, 2)).astype(bf)
        c["cf_" + sfx] = fwd_arr(C)
        c["sf_" + sfx] = fwd_arr(Sfm)
        c["ci_" + sfx] = inv_arr(C)
        c["si_" + sfx] = inv_arr(Sim)
    return c


def kernel(x, c, ctx, c_ctx, ada_w, ada_b, norm_g, w_in, w_out, da_lambda, da_subln_g,
           gq_q_g, gq_k_g, gq_out_g, hy_short_w, hy_short_b, hy_w1, hy_b1, hy_w2, hy_b2,
           hy_w3, hy_b3, hy_w4, hy_freq, hy_bias, hy_out_g, final_g):
    f = lambda a: np.ascontiguousarray(np.asarray(a, dtype=np.float32))
    x, c, ctx, c_ctx = f(x), f(c), f(ctx), f(c_ctx)
    shared = dict(
        cc_t=_colmajor(c_ctx),
        ada_w=f(ada_w), ada_b=f(ada_b), norm_g=f(norm_g), w_in=f(w_in), w_out=f(w_out),
        lamv=f(da_lambda),
        cols=np.stack([np.concatenate([
            f(gq_q_g)[l].reshape(128, 1), f(gq_k_g)[l].reshape(128, 1), f(da_subln_g)[l].reshape(128, 1),
            _colmajor(f(gq_out_g)[l]), _colmajor(f(hy_bias)[l]), _colmajor(f(hy_out_g)[l]),
            _colmajor(f(hy_short_b)[l]), np.zeros((128, 1), np.float32)], axis=1) for l in range(DEPTH)]),
        hy_sw=np.stack([np.ascontiguousarray(f(hy_short_w)[l].T.reshape(12, 128, 3).transpose(1, 0, 2)) for l in range(DEPTH)]),
        hy_w1=f(hy_w1), hy_w2=f(hy_w2), hy_w3=f(hy_w3), hy_w4=f(hy_w4),
        hy_b=np.ascontiguousarray(np.stack([f(hy_b1), f(hy_b2), f(hy_b3), f(hy_freq)], axis=-1)),
        final_g=f(final_g),
    )
    shared.update(_consts())
    in_maps = []
    for b in range(NCORES):
        m = dict(shared)
        m["xin"] = np.ascontiguousarray(np.concatenate([ctx[b], x[b]], axis=0))
        m["c_t"] = _colmajor(c[b])
        in_maps.append(m)
    nc = build(NLAYERS)
    res = run_bass_kernel_spmd(nc, in_maps, core_ids=list(range(NCORES)))
    global _last_results
    _last_results = res.results
    return np.stack([np.asarray(r["out"], dtype=np.float32) for r in res.results], axis=0)
```
